# Optimizing a Trainium2 kernel written in Bass

```python
import math
import jax, jax.numpy as jnp
from jax import lax
import numpy as np

D_MODEL = 1024
BATCH = 32
SEQ = 2048
DEPTH = 4
DEC_BATCH = 4
DEC_SEQ = 4096
PAST_LEN = 128

MEM_LEN = 256
CONV_W = D_MODEL // 2
CONV_K = 31
DA_HEADS = 4
DA_HD = D_MODEL // 16
DA_WIDTH = DA_HEADS * 2 * DA_HD
ROT_DIM = DA_HD // 4
ROPE_THETA = 500000.0
QBLK = 128
HG_HEADS = 4
HG_DK = D_MODEL // 8
HG_DV = D_MODEL // 8
HG_WIDTH = HG_HEADS * HG_DK
CHUNK = 64
GATE_FLOOR = 1e-30
N_BRANCH = 3
IN_SIZES = (CONV_W, CONV_W, DA_WIDTH, DA_WIDTH, DA_WIDTH, HG_WIDTH, HG_WIDTH, HG_WIDTH,
            HG_HEADS * HG_DV, HG_HEADS * HG_DV, N_BRANCH * D_MODEL)
N_IN = sum(IN_SIZES)
X_HEADS = 4
X_HD = D_MODEL // X_HEADS
D_FF = 2816
FFN_K = 3
EPS = 1e-6

kernel_name = "hybrid_conv_diffattn_hgrn2_encoder"

F32 = jnp.float32


def rmsnorm(x, g):
    xf = x.astype(F32)
    y = xf * lax.rsqrt(jnp.mean(xf * xf, axis=-1, keepdims=True) + EPS)
    return (y * g.astype(F32)).astype(x.dtype)


def layernorm(x, g, b):
    xf = x.astype(F32)
    mu = jnp.mean(xf, axis=-1, keepdims=True)
    var = jnp.mean(jnp.square(xf - mu), axis=-1, keepdims=True)
    return ((xf - mu) * lax.rsqrt(var + EPS) * g.astype(F32) + b.astype(F32)).astype(x.dtype)


def dwconv(x, w, b):
    C = x.shape[-1]
    y = lax.conv_general_dilated(x, w[:, None, :].astype(x.dtype), window_strides=(1,), padding="SAME",
                                 dimension_numbers=("NWC", "WIO", "NWC"), feature_group_count=C)
    return y + b.astype(x.dtype)


def rope_tables(S, dtype):
    inv = 1.0 / (ROPE_THETA ** (jnp.arange(0, ROT_DIM, 2, dtype=F32) / ROT_DIM))
    ang = jnp.arange(S, dtype=F32)[:, None] * inv[None, :]
    return jnp.cos(ang).astype(dtype), jnp.sin(ang).astype(dtype)


def apply_partial_rope(x, cos, sin):
    half = ROT_DIM // 2
    c = cos[None, :, None, :]
    s = sin[None, :, None, :]
    x1 = x[..., :half]
    x2 = x[..., half:ROT_DIM]
    return jnp.concatenate([x1 * c - x2 * s, x2 * c + x1 * s, x[..., ROT_DIM:]], axis=-1)


def conv_branch(a, b, dw_w, dw_b, ln_g, ln_b, w_out):
    c = a * jax.nn.sigmoid(b)
    c = dwconv(c, dw_w, dw_b)
    c = jax.nn.silu(layernorm(c, ln_g, ln_b))
    return c @ w_out


def diff_attention_branch(aq, ak, av, lam_params, subln_g, w_out, layer):
    B, S, _ = aq.shape
    cos, sin = rope_tables(S, aq.dtype)
    q = apply_partial_rope(aq.reshape(B, S, DA_HEADS * 2, DA_HD), cos, sin).reshape(B, S, DA_HEADS, 2, DA_HD)
    k = apply_partial_rope(ak.reshape(B, S, DA_HEADS * 2, DA_HD), cos, sin).reshape(B, S, DA_HEADS, 2, DA_HD)
    v = av.reshape(B, S, DA_HEADS, 2 * DA_HD)
    lam_init = 0.8 - 0.6 * math.exp(-0.3 * layer)
    lp = lam_params.astype(F32)
    lam = jnp.exp(jnp.sum(lp[0] * lp[1])) - jnp.exp(jnp.sum(lp[2] * lp[3])) + lam_init
    scale = DA_HD ** -0.5
    nb = S // QBLK
    qb = q.reshape(B, nb, QBLK, DA_HEADS, 2, DA_HD).transpose(1, 0, 2, 3, 4, 5)

    def block(qi):
        s = jnp.einsum("bqhcd,bkhcd->bhcqk", qi, k).astype(F32) * scale
        p = jax.nn.softmax(s, axis=-1)
        w = p[:, :, 0] - lam * p[:, :, 1]
        return jnp.einsum("bhqk,bkhe->bqhe", w.astype(v.dtype), v)

    o = lax.map(block, qb)
    o = o.transpose(1, 0, 2, 3, 4).reshape(B, S, DA_HEADS, 2 * DA_HD)
    o = rmsnorm(o, subln_g) * (1.0 - lam_init)
    return o.reshape(B, S, DA_WIDTH) @ w_out


def hgrn_chunk_scan(q, k, v, g):
    B, S, H, K = q.shape
    V = v.shape[-1]
    n = S // CHUNK

    def to_chunks(t):
        return t.reshape(B, n, CHUNK, H, t.shape[-1]).transpose(1, 0, 3, 2, 4)

    mask = jnp.tril(jnp.ones((CHUNK, CHUNK), dtype=bool))[:, :, None]

    def step(state, inp):
        qc, kc, vc, gc = inp
        G = jnp.cumsum(gc, axis=2)
        inter = jnp.einsum("bhtk,bhkv->bhtv", qc * jnp.exp(G), state)
        rel = G[:, :, :, None, :] - G[:, :, None, :, :]
        decay = jnp.where(mask, jnp.exp(jnp.where(mask, rel, 0.0)), 0.0)
        A = jnp.einsum("bhtk,bhsk,bhtsk->bhts", qc, kc, decay)
        intra = jnp.einsum("bhts,bhsv->bhtv", A, vc)
        G_end = G[:, :, -1]
        new_state = jnp.exp(G_end)[..., None] * state + jnp.einsum(
            "bhsk,bhsv->bhkv", kc * jnp.exp(G_end[:, :, None, :] - G), vc)
        return new_state, inter + intra

    s0 = jnp.zeros((B, H, K, V), F32)
    _, o = lax.scan(step, s0, (to_chunks(q), to_chunks(k), to_chunks(v), to_chunks(g)))
    return o.transpose(1, 0, 3, 2, 4).reshape(B, S, H, V)


def hgrn2_branch(hq, hf_fwd, hf_bwd, hi, hgate, lb_dirs, norm_g, w_out):
    B, S, _ = hq.shape
    q = jax.nn.silu(hq.astype(F32)).reshape(B, S, HG_HEADS, HG_DK)
    v = hi.astype(F32).reshape(B, S, HG_HEADS, HG_DV)

    def gates(f_logit, lb):
        f = f_logit.astype(F32).reshape(B, S, HG_HEADS, HG_DK)
        lb = lb.reshape(HG_HEADS, HG_DK)
        forget = lb + (1.0 - lb) * jax.nn.sigmoid(f)
        g = jnp.log(jnp.maximum(forget, GATE_FLOOR))
        kk = 1.0 - forget
        return kk, g

    k_f, g_f = gates(hf_fwd, lb_dirs[0])
    k_b, g_b = gates(hf_bwd, lb_dirs[1])
    o_f = hgrn_chunk_scan(q, k_f, v, g_f)
    o_b = jnp.flip(hgrn_chunk_scan(jnp.flip(q, 1), jnp.flip(k_b, 1), jnp.flip(v, 1), jnp.flip(g_b, 1)), 1)
    o = rmsnorm(o_f + o_b, norm_g) * jax.nn.silu(hgate.astype(F32).reshape(B, S, HG_HEADS, HG_DV))
    return o.reshape(B, S, HG_WIDTH).astype(hq.dtype) @ w_out


def cross_attention(hn, mn, w_q, w_kv, w_o):
    B, S, _ = hn.shape
    M = mn.shape[1]
    q = (hn @ w_q).reshape(B, S, X_HEADS, X_HD)
    kv = mn @ w_kv
    k = kv[..., :D_MODEL].reshape(B, M, X_HEADS, X_HD)
    v = kv[..., D_MODEL:].reshape(B, M, X_HEADS, X_HD)
    s = jnp.einsum("bqhd,bkhd->bhqk", q, k).astype(F32) * (X_HD ** -0.5)
    p = jax.nn.softmax(s, axis=-1).astype(v.dtype)
    o = jnp.einsum("bhqk,bkhd->bqhd", p, v).reshape(B, S, D_MODEL)
    return o @ w_o


def conv_ffn(hn, w_up, dw_w, dw_b, w_down):
    u = dwconv(hn @ w_up, dw_w, dw_b)
    a, b = jnp.split(u, 2, axis=-1)
    return (jax.nn.silu(a) * b) @ w_down


def encoder_trunk(x, mem, p, lbs):
    B, S, _ = x.shape
    split_points = [int(s) for s in np.cumsum(IN_SIZES)[:-1]]
    for l in range(DEPTH):
        xn = rmsnorm(x, p["norm_mix_g"][l])
        proj = xn @ p["w_in"][l]
        ca, cb, aq, ak, av, hq, hff, hfb, hi, hg, gl = jnp.split(proj, split_points, axis=-1)
        y_c = conv_branch(ca, cb, p["conv_dw_w"][l], p["conv_dw_b"][l], p["conv_ln_g"][l],
                          p["conv_ln_b"][l], p["w_conv_out"][l])
        y_a = diff_attention_branch(aq, ak, av, p["attn_lambda"][l], p["attn_subln_g"][l],
                                    p["w_attn_out"][l], l)
        y_h = hgrn2_branch(hq, hff, hfb, hi, hg, lbs[l], p["hg_norm_g"][l], p["w_hg_out"][l])
        gt = jax.nn.sigmoid(gl.astype(F32)).astype(x.dtype).reshape(B, S, N_BRANCH, D_MODEL)
        merged = gt[:, :, 0] * y_c + gt[:, :, 1] * y_a + gt[:, :, 2] * y_h
        x = x + merged @ p["w_o"][l]
        hn = rmsnorm(x, p["norm_cross_g"][l])
        mn = rmsnorm(mem, p["norm_mem_g"][l])
        x = x + cross_attention(hn, mn, p["w_cq"][l], p["w_ckv"][l], p["w_co"][l])
        hn = rmsnorm(x, p["norm_ffn_g"][l])
        x = x + conv_ffn(hn, p["w_up"][l], p["ffn_dw_w"][l], p["ffn_dw_b"][l], p["w_down"][l])
    return rmsnorm(x, p["final_norm_g"])


def setup_inputs(seed: int = 0) -> dict:
    key = jax.random.key(seed)
    ks = jax.random.split(key, 40)
    it = iter(range(40))

    def nrm(shape, scale):
        return scale * jax.random.normal(ks[next(it)], shape, F32)

    def gain(shape):
        return 1.0 + nrm(shape, 0.02)

    L, D = DEPTH, D_MODEL
    return {
        "x_prompt": nrm((BATCH, SEQ, D), 1.0),
        "x_sample": nrm((DEC_BATCH, DEC_SEQ, D), 1.0),
        "mem_prompt": nrm((BATCH, MEM_LEN, D), 1.0),
        "mem_sample": nrm((DEC_BATCH, MEM_LEN, D), 1.0),
        "norm_mix_g": gain((L, D)),
        "w_in": nrm((L, D, N_IN), D ** -0.5),
        "conv_dw_w": nrm((L, CONV_K, CONV_W), CONV_K ** -0.5),
        "conv_dw_b": nrm((L, CONV_W), 0.02),
        "conv_ln_g": gain((L, CONV_W)),
        "conv_ln_b": nrm((L, CONV_W), 0.02),
        "w_conv_out": nrm((L, CONV_W, D), CONV_W ** -0.5),
        "attn_lambda": nrm((L, 4, DA_HD), 0.1),
        "attn_subln_g": gain((L, 2 * DA_HD)),
        "w_attn_out": nrm((L, DA_WIDTH, D), DA_WIDTH ** -0.5),
        "hg_lb_param": nrm((L, 2, HG_WIDTH), 0.1),
        "hg_norm_g": gain((L, HG_DV)),
        "w_hg_out": nrm((L, HG_WIDTH, D), HG_WIDTH ** -0.5),
        "w_o": nrm((L, D, D), D ** -0.5),
        "norm_cross_g": gain((L, D)),
        "norm_mem_g": gain((L, D)),
        "w_cq": nrm((L, D, D), D ** -0.5),
        "w_ckv": nrm((L, D, 2 * D), D ** -0.5),
        "w_co": nrm((L, D, D), D ** -0.5),
        "norm_ffn_g": gain((L, D)),
        "w_up": nrm((L, D, 2 * D_FF), D ** -0.5),
        "ffn_dw_w": nrm((L, FFN_K, 2 * D_FF), FFN_K ** -0.5),
        "ffn_dw_b": nrm((L, 2 * D_FF), 0.02),
        "w_down": nrm((L, D_FF, D), D_FF ** -0.5),
        "final_norm_g": gain((D,)),
    }


def reference(x_prompt, x_sample, mem_prompt, mem_sample, norm_mix_g, w_in, conv_dw_w, conv_dw_b,
              conv_ln_g, conv_ln_b, w_conv_out, attn_lambda, attn_subln_g, w_attn_out, hg_lb_param,
              hg_norm_g, w_hg_out, w_o, norm_cross_g, norm_mem_g, w_cq, w_ckv, w_co, norm_ffn_g,
              w_up, ffn_dw_w, ffn_dw_b, w_down, final_norm_g):
    p = {
        "norm_mix_g": norm_mix_g, "w_in": w_in, "conv_dw_w": conv_dw_w, "conv_dw_b": conv_dw_b,
        "conv_ln_g": conv_ln_g, "conv_ln_b": conv_ln_b, "w_conv_out": w_conv_out,
        "attn_lambda": attn_lambda, "attn_subln_g": attn_subln_g, "w_attn_out": w_attn_out,
        "hg_norm_g": hg_norm_g, "w_hg_out": w_hg_out, "w_o": w_o,
        "norm_cross_g": norm_cross_g, "norm_mem_g": norm_mem_g, "w_cq": w_cq, "w_ckv": w_ckv,
        "w_co": w_co, "norm_ffn_g": norm_ffn_g, "w_up": w_up, "ffn_dw_w": ffn_dw_w,
        "ffn_dw_b": ffn_dw_b, "w_down": w_down, "final_norm_g": final_norm_g,
    }
    lp = jax.nn.softmax(hg_lb_param.astype(F32), axis=0)
    lbs = jnp.cumsum(lp, axis=0) - lp[0:1]
    y_prompt = encoder_trunk(x_prompt, mem_prompt, p, lbs)
    y_sample = encoder_trunk(x_sample, mem_sample, p, lbs)
    return (y_prompt, y_sample)
```

```python
import math
import numpy as np
import concourse.bass as bass
import concourse.mybir as mybir
from concourse.bass_utils import run_bass_kernel_spmd
from contextlib import ExitStack

F32 = mybir.dt.float32
BF16 = mybir.dt.bfloat16
AF = mybir.ActivationFunctionType
ALU = mybir.AluOpType

D = 1024
MEM = 256
DFF = 2816
EPS = 1e-6
WSHAPES = {"w_in": (1024, 8192), "w_conv_out": (512, 1024), "w_attn_out": (512, 1024), "w_hg_out": (512, 1024),
           "w_o": (1024, 1024), "w_cq": (1024, 1024), "w_ckv": (1024, 2048), "w_co": (1024, 1024),
           "w_up": (1024, 5632), "w_down": (2816, 1024)}
SMALL = {"norm_mix_g": (1024,), "conv_dw_w": (31, 512), "conv_dw_b": (512,), "conv_ln_g": (512,), "conv_ln_b": (512,),
         "attn_lambda": (4, 64), "attn_subln_g": (128,), "hg_norm_g": (128,), "norm_cross_g": (1024,),
         "norm_mem_g": (1024,), "norm_ffn_g": (1024,), "ffn_dw_w": (3, 5632), "ffn_dw_b": (5632,)}


class Buf:
    __slots__ = ("w", "r", "name", "ap", "psum")

    def __init__(self, name="", ap=None, psum=False):
        self.w = []
        self.r = []
        self.name = name
        self.ap = ap
        self.psum = psum


class Sched:
    NDMA = 24

    def __init__(self, nc, es):
        self.nc = nc
        self.es = es
        self.eng = {"pe": nc.tensor, "act": nc.scalar, "dve": nc.vector, "pool": nc.gpsimd, "sp": nc.sync}
        self.sem = {}
        self.cnt = {}
        for k in self.eng:
            self.sem[k] = es.enter_context(nc.semaphore("s_" + k))
            self.cnt[k] = 0
        for q in ("sp", "pool", "act"):
            for i in range(self.NDMA):
                self.sem[("d" + q, i)] = es.enter_context(nc.semaphore("d%s%d" % (q, i)))
                self.cnt[("d" + q, i)] = 0
        self.dnext = {"sp": 0, "pool": 0, "act": 0}
        self.seen = {k: {} for k in self.eng}
        self.ninst = 0

    def sb(self, name, shape, dt, es=None):
        self.uid = getattr(self, "uid", 0) + 1
        name = "%s_%d" % (name, self.uid)
        t = (es or self.es).enter_context(self.nc.sbuf_tensor(name, shape, dt))
        return Buf(name, t.ap())

    def ps(self, name, shape, dt, es=None):
        t = (es or self.es).enter_context(self.nc.psum_tensor(name, shape, dt))
        return Buf(name, t.ap(), psum=True)

    def _wait(self, e, tok):
        key, val = tok
        if key == e and (e == "pe" or val > self.cnt[e]):
            return
        if self.seen[e].get(key, 0) >= val:
            return
        self.eng[e].wait_ge(self.sem[key], val)
        self.seen[e][key] = val
        self.ninst += 1

    def _deps(self, e, reads, writes):
        for b in reads:
            for tok in b.w:
                self._wait(e, tok)
            if b.psum:
                for tok in b.r:
                    if tok[0] != e:
                        self._wait(e, tok)
        for b in writes:
            for tok in b.w:
                self._wait(e, tok)
            for tok in b.r:
                self._wait(e, tok)

    def _commit(self, tok, reads, writes, partial):
        for b in writes:
            if partial:
                b.w = [t for t in b.w if t[0] != tok[0]]
                b.w.append(tok)
            else:
                b.w = [tok]
                b.r = []
        for b in reads:
            b.r = [t for t in b.r if t[0] != tok[0]]
            b.r.append(tok)

    def op(self, e, fn, reads=(), writes=(), inc=True, partial=False):
        self._deps(e, reads, writes)
        ins = fn(self.eng[e])
        self.ninst += 1
        if inc:
            self.cnt[e] += 1
            ins.then_inc(self.sem[e], 1)
            tok = (e, self.cnt[e])
        else:
            tok = (e, self.cnt[e] + 1)
        self._commit(tok, reads, writes, partial)
        return ins

    def pe(self, fn, **kw): return self.op("pe", fn, **kw)
    def act(self, fn, **kw): return self.op("act", fn, **kw)
    def dve(self, fn, **kw): return self.op("dve", fn, **kw)
    def pool(self, fn, **kw): return self.op("pool", fn, **kw)

    def dma(self, q, out, in_, reads=(), writes=(), partial=False, **kw):
        i = self.dnext[q]
        self.dnext[q] = (i + 1) % self.NDMA
        key = ("d" + q, i)
        if self.cnt[key] > 0:
            self._wait(q, (key, self.cnt[key]))
        self._deps(q, reads, writes)
        ins = self.eng[q].dma_start(out=out, in_=in_, **kw)
        self.ninst += 1
        self.cnt[key] += 16
        ins.then_inc(self.sem[key], 16)
        self._commit((key, self.cnt[key]), reads, writes, partial)

    def barrier(self):
        toks = [(k, v) for k, v in self.cnt.items() if v > 0]
        for e in self.eng:
            for tok in toks:
                if tok[0] != e:
                    self._wait(e, tok)

    def mm(self, out_ap, pairs, reads, out_buf):
        n = len(pairs)
        for i, (l, r) in enumerate(pairs):
            self.pe(lambda e, l=l, r=r, i=i: e.matmul(out_ap, lhsT=l, rhs=r, start=(i == 0), stop=(i == n - 1)),
                    reads=reads, writes=[out_buf], inc=(i == n - 1))


def build(seq_lens, depth, flags=("conv", "attn", "hgrn", "cross", "ffn")):
    nc = bass.Bass("TRN2", target_bir_lowering=False)
    nseq = len(seq_lens)
    NT = sum(seq_lens)
    SMAX = max(seq_lens)
    tok0 = [sum(seq_lens[:i]) for i in range(nseq)]
    hcol0 = [sum(s + 2 for s in seq_lens[:i]) + 1 for i in range(nseq)]
    HW = sum(s + 2 for s in seq_lens)
    L = depth

    def din(name, shape, dt=F32):
        return nc.dram_tensor(name, list(shape), dt, kind="ExternalInput").ap()

    def dscr(name, shape, dt):
        return nc.dram_tensor(name, list(shape), dt, kind="Internal").ap()

    x_in = din("x", [NT, D])
    mem_in = din("mem", [nseq * MEM, D])
    y_out = nc.dram_tensor("y", [NT, D], F32, kind="ExternalOutput").ap()
    TILED = {"w_in": 64, "w_up": 44}
    wshape = {k: ((L, TILED[k], 128, 1024) if k in TILED else (L,) + v) for k, v in WSHAPES.items()}
    wf = {k: din(k, wshape[k]) for k in WSHAPES}
    sm = {k: din(k, (L,) + v) for k, v in SMALL.items()}
    lb_in = din("hg_lb_param", (L, 2, 512))
    fin_g = din("final_norm_g", (1, D))
    ropeC = din("ropeC", (128, SMAX))
    ropeS = din("ropeS", (128, SMAX))
    permM = din("permM", (128, 128))
    wb = {k: dscr(k + "_b", wshape[k], BF16) for k in WSHAPES}

    def wtile(key, l, c):
        return wb[key][l, c].rearrange("p (k n) -> p k n", k=8)
    xs = dscr("xs", [NT, D], F32)
    hT = [dscr("hT%d" % i, [D, HW], BF16) for i in range(3)]
    zT = [dscr("zT%d" % i, [512, NT], BF16) for i in range(3)]
    d_wb = {(k, l): Buf() for k in WSHAPES for l in range(L)}
    d_xs = Buf("xs")
    d_h = [Buf("h%d" % i) for i in range(3)]
    d_z = [Buf("z%d" % i) for i in range(3)]
    d_y = Buf("y")

    with ExitStack() as es:
        S = Sched(nc, es)
        identB = S.sb("identB", [128, 128], BF16)
        identF = S.sb("identF", [128, 128], F32)
        onesB = S.sb("onesB", [128, 128], BF16)
        onesF = S.sb("onesF", [128, 128], F32)
        onesF512 = S.sb("onesF512", [128, 128], F32)
        onesF1 = S.sb("onesF1", [128, 128], F32)
        onesB128 = S.sb("onesB128", [128, 128], BF16)
        onesB512 = S.sb("onesB512", [128, 128], BF16)
        nhalf = S.sb("nhalf", [128, 8], F32)
        epsc = S.sb("epsc", [128, 1], F32)
        permB = S.sb("permB", [128, 128], BF16)
        maskF = S.sb("maskF", [128, 64], F32)
        maskBk = S.sb("maskBk", [128, 64], F32)
        zrow = S.sb("zrow", [128, 16], BF16)
        lbs = S.sb("lbs", [128, L, 2, 4], F32)
        omlb = S.sb("omlb", [128, L, 2, 4], F32)
        P = [S.ps("P%d" % i, [128, 512], F32) for i in range(6)]
        T = [S.ps("T%d" % i, [128, 1024], BF16) for i in range(2)]

        for b, v in ((identB, 1.0), (identF, 1.0), (onesB, 1.0), (onesF, 1.0 / 128), (onesF512, 1.0 / 512), (onesF1, 1.0), (onesB128, 1.0 / 128), (onesB512, 1.0 / 512),
                     (nhalf, -0.5), (epsc, EPS), (maskF, 1.0), (maskBk, 1.0), (zrow, 0.0)):
            S.pool(lambda e, b=b, v=v: e.memset(b.ap, v), writes=[b])
        for b in (identB, identF):
            S.pool(lambda e, b=b: e.affine_select(out=b.ap, in_=b.ap, pattern=[[-1, 128]], compare_op=ALU.is_equal,
                                                  fill=0.0, base=0, channel_multiplier=1), reads=[b], writes=[b])
        for h in range(2):
            S.pool(lambda e, h=h: e.affine_select(out=maskF.ap[h * 64:(h + 1) * 64, :], in_=maskF.ap[h * 64:(h + 1) * 64, :],
                                                  pattern=[[1, 64]], compare_op=ALU.is_ge, fill=0.0, base=0,
                                                  channel_multiplier=-1), reads=[maskF], writes=[maskF])
            S.pool(lambda e, h=h: e.affine_select(out=maskBk.ap[h * 64:(h + 1) * 64, :], in_=maskBk.ap[h * 64:(h + 1) * 64, :],
                                                  pattern=[[-1, 64]], compare_op=ALU.is_ge, fill=0.0, base=0,
                                                  channel_multiplier=1), reads=[maskBk], writes=[maskBk])
        S.dma("pool", permB.ap, permM, writes=[permB])
        for i in range(3):
            for s in range(nseq):
                for c in (hcol0[s] - 1, hcol0[s] + seq_lens[s]):
                    S.dma("pool", hT[i][:, c:c + 1].rearrange("(c p) n -> p c n", p=128),
                          zrow.ap[:, 0:8].rearrange("p (c n) -> p c n", n=1),
                          reads=[zrow], writes=[d_h[i]], partial=True, allow_slow_non_contiguous=True)
        def cast_layer(l):
            for k, (r, c) in WSHAPES.items():
                if k in TILED:
                    for c0 in range(0, TILED[k], 8):
                        c1 = min(TILED[k], c0 + 8)
                        S.dma("pool", wb[k][l, c0:c1], wf[k][l, c0:c1], writes=[d_wb[(k, l)]], partial=True)
                    continue
                step = 2048
                for c0 in range(0, c, step):
                    c1 = min(c, c0 + step)
                    S.dma("pool", wb[k][l, :, c0:c1], wf[k][l, :, c0:c1], writes=[d_wb[(k, l)]], partial=True)

        def cols_from_rows(es_, name, src2d, R, N, scale=None):
            nchk = N // 128
            out = S.sb(name, [128, nchk, R], F32, es_)
            er = ExitStack()
            rows = S.sb(name + "_r", [R, N], F32, er)
            S.dma("sp", rows.ap, src2d, writes=[rows])
            for c in range(nchk):
                S.pe(lambda e, c=c: e.matmul(P[5].ap[:, 0:R], lhsT=rows.ap[0:R, c * 128:(c + 1) * 128],
                                            rhs=identF.ap[0:R, 0:R], start=True, stop=True),
                     reads=[rows, identF], writes=[P[5]])
                if scale is None:
                    S.dve(lambda e, c=c: e.tensor_copy(out=out.ap[:, c, :], in_=P[5].ap[:, 0:R]), reads=[P[5]],
                          writes=[out], partial=True)
                else:
                    S.dve(lambda e, c=c: e.tensor_scalar(out=out.ap[:, c, :], in0=P[5].ap[:, 0:R], scalar1=scale,
                                                         scalar2=None, op0=ALU.mult), reads=[P[5]], writes=[out],
                          partial=True)
            S.barrier()
            er.close()
            return out

        with ExitStack() as e1:
            lbc = cols_from_rows(e1, "lbc", lb_in.rearrange("l t n -> (l t) n"), 2 * L, 512)
            ex = S.sb("lb_ex", [128, 4, 2 * L], F32, e1)
            sm_ = S.sb("lb_sm", [128, 4, 2], F32, e1)
            S.act(lambda e: e.activation(out=ex.ap, in_=lbc.ap, func=AF.Exp), reads=[lbc], writes=[ex])
            exv = ex.ap.rearrange("p c (l t) -> p c l t", t=2)
            S.dve(lambda e: e.tensor_copy(out=sm_.ap, in_=exv[:, :, 0, :]), reads=[ex], writes=[sm_])
            for l in range(1, L):
                S.dve(lambda e, l=l: e.tensor_tensor(out=sm_.ap, in0=sm_.ap, in1=exv[:, :, l, :], op=ALU.add),
                      reads=[ex, sm_], writes=[sm_])
            S.dve(lambda e: e.reciprocal(out=sm_.ap, in_=sm_.ap), reads=[sm_], writes=[sm_])
            lbv = lbs.ap.rearrange("p l t c -> p c l t")
            S.dve(lambda e: e.memset(lbv[:, :, 0, :], 0.0), writes=[lbs])
            for l in range(1, L):
                S.dve(lambda e, l=l: e.tensor_tensor(out=lbv[:, :, l, :], in0=exv[:, :, l, :], in1=sm_.ap, op=ALU.mult),
                      reads=[ex, sm_, lbs], writes=[lbs])
                if l > 1:
                    S.dve(lambda e, l=l: e.tensor_tensor(out=lbv[:, :, l, :], in0=lbv[:, :, l, :], in1=lbv[:, :, l - 1, :],
                                                         op=ALU.add), reads=[lbs], writes=[lbs])
            S.dve(lambda e: e.tensor_scalar(out=omlb.ap, in0=lbs.ap, scalar1=-1.0, scalar2=1.0, op0=ALU.mult, op1=ALU.add),
                  reads=[lbs], writes=[omlb])
            S.barrier()

        def norm_T(xt, gbc, stage, col, tmp):
            junk, ss, xn = tmp
            S.act(lambda e: e.activation(out=junk.ap, in_=xt.ap, func=AF.Square, accum_out=ss.ap), reads=[xt],
                  writes=[junk, ss])
            S.act(lambda e: e.activation(out=ss.ap, in_=ss.ap, func=AF.Sqrt, scale=1.0 / D, bias=epsc.ap[:, 0:1]),
                  reads=[ss, epsc], writes=[ss])
            S.dve(lambda e: e.reciprocal(out=ss.ap, in_=ss.ap), reads=[ss], writes=[ss])
            S.dve(lambda e: e.scalar_tensor_tensor(out=xn.ap, in0=xt.ap, scalar=ss.ap, in1=gbc.ap, op0=ALU.mult,
                                                   op1=ALU.mult), reads=[xt, ss, gbc], writes=[xn])
            return xn

        def transpose_to(xn, stage, col, tb):
            for c in range(8):
                S.pe(lambda e, c=c: e.transpose(out=tb.ap[:, c * 128:(c + 1) * 128], in_=xn.ap[:, c * 128:(c + 1) * 128],
                                                identity=identB.ap), reads=[xn, identB], writes=[tb], inc=(c == 7))
            S.act(lambda e: e.copy(out=stage.ap[:, :, col:col + 128], in_=tb.ap.rearrange("p (c n) -> p c n", c=8)),
                  reads=[tb], writes=[stage], partial=True)

        def load_gbc(es_, name, src_row):
            g = S.sb(name, [128, D], F32, es_)
            S.dma("sp", g.ap, src_row.partition_broadcast(128), writes=[g])
            return g

        def load_w(dst, key, l, c0, c1, r0=0, r1=None):
            r1 = WSHAPES[key][0] if r1 is None else r1
            S.dma("sp", dst, wb[key][l, r0:r1, c0:c1].rearrange("(c p) n -> p c n", p=128), reads=[d_wb[(key, l)]],
                  writes=[dst_b[0]])

        dst_b = [None]

        def ldw(dstbuf, dst_ap, key, l, c0, c1, r0=0, r1=None):
            dst_b[0] = dstbuf
            load_w(dst_ap, key, l, c0, c1, r0, r1)

        def phase_prologue():
            with ExitStack() as e1:
                gbc = load_gbc(e1, "gbc", sm["norm_mix_g"][0:1, :])
                xt = [S.sb("pxt%d" % i, [128, D], F32, e1) for i in range(2)]
                tmp = [(S.sb("pj%d" % i, [128, D], F32, e1), S.sb("pss%d" % i, [128, 1], F32, e1),
                        S.sb("pxn%d" % i, [128, D], BF16, e1)) for i in range(2)]
                stage = [S.sb("pst%d" % i, [128, 8, 512], BF16, e1) for i in range(2)]
                it = 0
                for s in range(nseq):
                    for j in range(seq_lens[s] // 512):
                        st = stage[j % 2]
                        for m in range(4):
                            t0 = tok0[s] + j * 512 + m * 128
                            b = it % 2
                            it += 1
                            S.dma("sp", xt[b].ap, x_in[t0:t0 + 128, :], writes=[xt[b]])
                            S.dma("pool", xs[t0:t0 + 128, :], xt[b].ap, reads=[xt[b]], writes=[d_xs], partial=True)
                            xn = norm_T(xt[b], gbc, st, m * 128, tmp[b])
                            transpose_to(xn, st, m * 128, T[b])
                        c0 = hcol0[s] + j * 512
                        S.dma("pool", hT[0][:, c0:c0 + 512].rearrange("(c p) n -> p c n", p=128), st.ap, reads=[st],
                              writes=[d_h[0]], partial=True)
                S.barrier()

        def residual_epilogue(ctx, t0, pa, pb, last_layer_final, st, col, scale=None):
            xt, gbc, tmp, tb, yt = ctx
            S.dma("sp", xt.ap, xs[t0:t0 + 128, :], reads=[d_xs], writes=[xt])
            for hh_, pp_ in ((0, pa), (1, pb)):
                sl_ = slice(hh_ * 512, (hh_ + 1) * 512)
                if scale is None:
                    S.dve(lambda e, sl_=sl_, pp_=pp_: e.tensor_tensor(out=xt.ap[:, sl_], in0=xt.ap[:, sl_], in1=pp_.ap, op=ALU.add),
                          reads=[xt, pp_], writes=[xt])
                else:
                    S.dve(lambda e, sl_=sl_, pp_=pp_: e.scalar_tensor_tensor(out=xt.ap[:, sl_], in0=pp_.ap, scalar=scale,
                                                                             in1=xt.ap[:, sl_], op0=ALU.mult, op1=ALU.add),
                          reads=[xt, pp_], writes=[xt])
            if not last_layer_final:
                S.dma("pool", xs[t0:t0 + 128, :], xt.ap, reads=[xt], writes=[d_xs], partial=True)
                xn = norm_T(xt, gbc, st, col, tmp)
                transpose_to(xn, st, col, tb)
            else:
                junk, ss, xn = tmp
                S.act(lambda e: e.activation(out=junk.ap, in_=xt.ap, func=AF.Square, accum_out=ss.ap), reads=[xt],
                      writes=[junk, ss])
                S.act(lambda e: e.activation(out=ss.ap, in_=ss.ap, func=AF.Sqrt, scale=1.0 / D, bias=epsc.ap[:, 0:1]),
                      reads=[ss, epsc], writes=[ss])
                S.dve(lambda e: e.reciprocal(out=ss.ap, in_=ss.ap), reads=[ss], writes=[ss])
                S.dve(lambda e: e.scalar_tensor_tensor(out=yt.ap, in0=xt.ap, scalar=ss.ap, in1=gbc.ap, op0=ALU.mult,
                                                       op1=ALU.mult), reads=[xt, ss, gbc], writes=[yt])
                S.dma("pool", y_out[t0:t0 + 128, :], yt.ap, reads=[yt], writes=[d_y], partial=True)

        def make_epi_ctx(e1, gsrc, n=2):
            gbc = load_gbc(e1, "egbc", gsrc)
            ctxs = []
            for i in range(n):
                junk_ = S.sb("ej%d" % i, [128, D], F32, e1)
                ctxs.append((S.sb("ext%d" % i, [128, D], F32, e1), gbc,
                             (junk_, S.sb("ess%d" % i, [128, 1], F32, e1),
                              S.sb("exn%d" % i, [128, D], BF16, e1)), T[i], junk_))
            return ctxs

        def phase_ffn(l):
            last = (l == L - 1)
            SEG = 1024 if all(sl % 1024 == 0 for sl in seq_lens) else 512
            cranges = [(c0, min(c0 + 512, SEG + 2)) for c0 in range(0, SEG + 2, 512)]
            bankc = [0]
            with ExitStack() as e1:
                gsrc = fin_g[0:1, :] if last else sm["norm_mix_g"][l + 1:l + 2, :]
                ctxs = make_epi_ctx(e1, gsrc)
                dww = cols_from_rows(e1, "fdw", sm["ffn_dw_w"][l], 3, 2 * DFF)
                dwb = cols_from_rows(e1, "fdb", sm["ffn_dw_b"][l:l + 1, :], 1, 2 * DFF)
                dwbh = S.sb("fdbh", [128, 44, 1], F32, e1)
                S.dve(lambda e: e.tensor_scalar(out=dwbh.ap, in0=dwb.ap, scalar1=0.5, scalar2=None, op0=ALU.mult),
                      reads=[dwb], writes=[dwbh])
                wdn = S.sb("wdn", [128, 22, D], BF16, e1)
                ldw(wdn, wdn.ap, "w_down", l, 0, D)
                wup = [S.sb("wup%d" % i, [128, 2, 8, 128], BF16, e1) for i in range(2)]
                hn = S.sb("fhn", [128, 8, SEG + 2], BF16, e1)
                hT_ = S.sb("fh", [128, 22, SEG], BF16, e1)
                pa = [S.sb("fpa%d" % i, [128, SEG + 2], F32, e1) for i in range(2)]
                ua = [S.sb("fua%d" % i, [128, SEG], F32, e1) for i in range(2)]
                th = S.sb("fth", [128, SEG], F32, e1)
                stage = S.sb("fst", [128, 8, SEG], BF16, e1)
                for s in range(nseq):
                    for g in range(seq_lens[s] // SEG):
                        c0 = hcol0[s] + g * SEG - 1
                        S.dma("sp", hn.ap, hT[2][:, c0:c0 + SEG + 2].rearrange("(c p) n -> p c n", p=128), reads=[d_h[2]],
                              writes=[hn])
                        for j in range(22):
                            w = wup[j % 2]
                            dst_b[0] = w
                            S.dma("sp", w.ap[:, 0], wtile("w_up", l, j), reads=[d_wb[("w_up", l)]], writes=[w])
                            S.dma("sp", w.ap[:, 1], wtile("w_up", l, 22 + j), reads=[d_wb[("w_up", l)]], writes=[w], partial=True)
                            for h in range(2):
                                ch = j + 22 * h
                                for ri, (r0, r1) in enumerate(cranges):
                                    pp = P[bankc[0] % 4]
                                    bankc[0] += 1
                                    S.mm(pp.ap[:, 0:r1 - r0], [(w.ap[:, h, k, :], hn.ap[:, k, r0:r1]) for k in range(8)],
                                         [w, hn], pp)
                                    S.act(lambda e, h=h, pp=pp, r0=r0, r1=r1: e.copy(out=pa[h].ap[:, r0:r1], in_=pp.ap[:, 0:r1 - r0]),
                                          reads=[pp], writes=[pa[h]], partial=(ri > 0))
                                S.act(lambda e, h=h, ch=ch: e.activation(
                                    out=ua[h].ap, in_=pa[h].ap[:, 1:SEG + 1], func=AF.Identity, scale=dww.ap[:, ch, 1:2],
                                    bias=dwb.ap[:, ch, 0:1]), reads=[pa[h], dww, dwb], writes=[ua[h]])
                                for tp in (0, 2):
                                    S.dve(lambda e, h=h, ch=ch, tp=tp: e.scalar_tensor_tensor(
                                        out=ua[h].ap, in0=pa[h].ap[:, tp:tp + SEG], scalar=dww.ap[:, ch, tp:tp + 1],
                                        in1=ua[h].ap, op0=ALU.mult, op1=ALU.add), reads=[pa[h], dww, ua[h]], writes=[ua[h]])
                            S.act(lambda e: e.activation(out=th.ap, in_=ua[0].ap, func=AF.Tanh, scale=0.5), reads=[ua[0]],
                                  writes=[th])
                            S.dve(lambda e: e.scalar_tensor_tensor(out=th.ap, in0=th.ap, scalar=1.0, in1=ua[0].ap,
                                                                   op0=ALU.add, op1=ALU.mult), reads=[th, ua[0]], writes=[th])
                            S.dve(lambda e, j=j: e.scalar_tensor_tensor(out=hT_.ap[:, j, :], in0=th.ap, scalar=0.5,
                                                                        in1=ua[1].ap, op0=ALU.mult, op1=ALU.mult),
                                  reads=[th, ua[1]], writes=[hT_], partial=True)
                        for m in range(SEG // 128):
                            t0 = tok0[s] + g * SEG + m * 128
                            ctx = ctxs[m % 2]
                            pa_, pb_ = P[4], P[5]
                            for hh, pp in ((0, pa_), (1, pb_)):
                                S.mm(pp.ap, [(hT_.ap[:, j, m * 128:(m + 1) * 128], wdn.ap[:, j, hh * 512:(hh + 1) * 512])
                                             for j in range(22)], [hT_, wdn], pp)
                            residual_epilogue(ctx, t0, pa_, pb_, last, stage, m * 128)
                        if not last:
                            cc = hcol0[s] + g * SEG
                            S.dma("pool", hT[0][:, cc:cc + SEG].rearrange("(c p) n -> p c n", p=128), stage.ap,
                                  reads=[stage], writes=[d_h[0]], partial=True)
                S.barrier()

        def phase_copy_h(src, dst):
            with ExitStack() as e1:
                t = S.sb("cph", [128, 8, 512], BF16, e1)
                for s in range(nseq):
                    for g in range(seq_lens[s] // 512):
                        c0 = hcol0[s] + g * 512
                        S.dma("sp", t.ap, hT[src][:, c0:c0 + 512].rearrange("(c p) n -> p c n", p=128), reads=[d_h[src]],
                              writes=[t])
                        S.dma("pool", hT[dst][:, c0:c0 + 512].rearrange("(c p) n -> p c n", p=128), t.ap, reads=[t],
                              writes=[d_h[dst]], partial=True)
                S.barrier()

        def phase_cross(l):
            SEG = 512
            with ExitStack() as e1:
                ctxs = make_epi_ctx(e1, sm["norm_ffn_g"][l:l + 1, :])
                gmem = load_gbc(e1, "gmem", sm["norm_mem_g"][l:l + 1, :])
                wcq = S.sb("wcq", [128, 8, D], BF16, e1)
                wco = S.sb("wco", [128, 8, D], BF16, e1)
                wkv = S.sb("wkv", [128, 8, 2 * D], BF16, e1)
                for wbuf, key in ((wcq, "w_cq"), (wco, "w_co"), (wkv, "w_ckv")):
                    S.dma("sp", wbuf.ap, wb[key][l].rearrange("(c p) n -> p c n", p=128), reads=[d_wb[(key, l)]],
                          writes=[wbuf])
                mnT = S.sb("mnT", [128, 8, MEM], BF16, e1)
                kT = S.sb("ckT", [128, 8, MEM], BF16, e1)
                vtok = S.sb("cvt", [128, 2, D], BF16, e1)
                hn = S.sb("chn", [128, 8, SEG], BF16, e1)
                qT = S.sb("cqT", [128, 8, SEG], BF16, e1)
                oT = S.sb("coT", [128, 8, SEG], BF16, e1)
                pr = [S.sb("cp%d" % i, [128, SEG], BF16, e1) for i in range(2)]
                rden = S.sb("crd", [128, SEG], F32, e1)
                stage = S.sb("cst", [128, 8, SEG], BF16, e1)
                for s in range(nseq):
                    for mm_ in range(2):
                        ctx = ctxs[mm_]
                        S.dma("sp", ctx[0].ap, mem_in[s * MEM + mm_ * 128:s * MEM + (mm_ + 1) * 128, :], writes=[ctx[0]])
                        xn = norm_T(ctx[0], gmem, None, 0, ctx[2])
                        transpose_to(xn, mnT, mm_ * 128, ctx[3])
                    for c in range(8):
                        S.mm(P[5].ap[:, 0:MEM], [(wkv.ap[:, k, c * 128:(c + 1) * 128], mnT.ap[:, k, :]) for k in range(8)],
                             [wkv, mnT], P[5])
                        S.act(lambda e, c=c: e.copy(out=kT.ap[:, c, :], in_=P[5].ap[:, 0:MEM]), reads=[P[5]], writes=[kT],
                              partial=True)
                    for mm_ in range(2):
                        for hh in range(2):
                            S.mm(P[4].ap, [(mnT.ap[:, k, mm_ * 128:(mm_ + 1) * 128],
                                            wkv.ap[:, k, D + hh * 512:D + (hh + 1) * 512]) for k in range(8)], [wkv, mnT], P[4])
                            S.act(lambda e, mm_=mm_, hh=hh: e.copy(out=vtok.ap[:, mm_, hh * 512:(hh + 1) * 512], in_=P[4].ap),
                                  reads=[P[4]], writes=[vtok], partial=True)
                    for g in range(seq_lens[s] // SEG):
                        c0 = hcol0[s] + g * SEG
                        S.dma("sp", hn.ap, hT[1][:, c0:c0 + SEG].rearrange("(c p) n -> p c n", p=128), reads=[d_h[1]],
                              writes=[hn])
                        for c in range(8):
                            S.mm(P[5].ap, [(wcq.ap[:, k, c * 128:(c + 1) * 128], hn.ap[:, k, :]) for k in range(8)],
                                 [wcq, hn], P[5])
                            S.act(lambda e, c=c: e.copy(out=qT.ap[:, c, :], in_=P[5].ap), reads=[P[5]], writes=[qT],
                                  partial=True)
                        for h in range(4):
                            for mm_ in range(2):
                                S.mm(P[mm_].ap, [(kT.ap[:, 2 * h + dc, mm_ * 128:(mm_ + 1) * 128], qT.ap[:, 2 * h + dc, :])
                                                 for dc in range(2)], [kT, qT], P[mm_])
                                S.act(lambda e, mm_=mm_: e.activation(out=pr[mm_].ap, in_=P[mm_].ap, func=AF.Exp,
                                                                      scale=1.0 / 16.0), reads=[P[mm_]], writes=[pr[mm_]])
                            for dc in range(2):
                                S.mm(P[2 + dc].ap, [(vtok.ap[:, mm_, (2 * h + dc) * 128:(2 * h + dc + 1) * 128], pr[mm_].ap)
                                                    for mm_ in range(2)], [vtok, pr[0], pr[1]], P[2 + dc])
                            S.mm(P[4].ap, [(onesB.ap, pr[mm_].ap) for mm_ in range(2)], [onesB, pr[0], pr[1]], P[4])
                            S.dve(lambda e: e.reciprocal(out=rden.ap, in_=P[4].ap), reads=[P[4]], writes=[rden])
                            for dc in range(2):
                                S.dve(lambda e, dc=dc, h=h: e.tensor_tensor(out=oT.ap[:, 2 * h + dc, :], in0=P[2 + dc].ap,
                                                                            in1=rden.ap, op=ALU.mult),
                                      reads=[P[2 + dc], rden], writes=[oT], partial=True)
                        for m in range(SEG // 128):
                            t0 = tok0[s] + g * SEG + m * 128
                            for hh in range(2):
                                S.mm(P[hh].ap, [(oT.ap[:, k, m * 128:(m + 1) * 128], wco.ap[:, k, hh * 512:(hh + 1) * 512])
                                                for k in range(8)], [oT, wco], P[hh])
                            residual_epilogue(ctxs[m % 2], t0, P[0], P[1], False, stage, m * 128)
                        S.dma("pool", hT[2][:, c0:c0 + SEG].rearrange("(c p) n -> p c n", p=128), stage.ap, reads=[stage],
                              writes=[d_h[2]], partial=True)
                S.barrier()

        def branch_conv(l, s, xnT, Sq):
            with ExitStack() as e1:
                cw = cols_from_rows(e1, "cw", sm["conv_dw_w"][l], 31, 512, scale=0.5)
                cbias = cols_from_rows(e1, "cbi", sm["conv_dw_b"][l:l + 1, :], 1, 512)
                lng = cols_from_rows(e1, "clg", sm["conv_ln_g"][l:l + 1, :], 1, 512, scale=0.5)
                lnb = cols_from_rows(e1, "clb", sm["conv_ln_b"][l:l + 1, :], 1, 512, scale=0.5)
                diag = S.sb("cdiag", [128, 4, 31, 128], BF16, e1)
                for i in range(4):
                    for tp in range(31):
                        S.dve(lambda e, i=i, tp=tp: e.tensor_scalar(out=diag.ap[:, i, tp, :], in0=identB.ap,
                                                                    scalar1=cw.ap[:, i, tp:tp + 1], scalar2=None,
                                                                    op0=ALU.mult), reads=[identB, cw], writes=[diag],
                              partial=True)
                cbuf = S.sb("cbuf", [128, 4, Sq + 30], BF16, e1)
                S.pool(lambda e: e.memset(cbuf.ap[:, :, 0:15], 0.0), writes=[cbuf])
                S.pool(lambda e: e.memset(cbuf.ap[:, :, Sq + 15:Sq + 30], 0.0), writes=[cbuf], partial=True)
                wab = [S.sb("cwab%d" % i, [128, 2, 8, 128], BF16, e1) for i in range(2)]
                th = S.sb("cth", [128, 512], F32, e1)
                for i in range(4):
                    w = wab[i % 2]
                    S.dma("sp", w.ap[:, 0], wtile("w_in", l, i), reads=[d_wb[("w_in", l)]], writes=[w])
                    S.dma("sp", w.ap[:, 1], wtile("w_in", l, 4 + i), reads=[d_wb[("w_in", l)]], writes=[w], partial=True)
                    for j in range(Sq // 512):
                        pa_, pb_ = P[(2 * j) % 4], P[(2 * j + 1) % 4]
                        S.mm(pa_.ap, [(w.ap[:, 0, k, :], xnT.ap[:, k, j * 512:(j + 1) * 512]) for k in range(8)], [w, xnT], pa_)
                        S.mm(pb_.ap, [(w.ap[:, 1, k, :], xnT.ap[:, k, j * 512:(j + 1) * 512]) for k in range(8)], [w, xnT], pb_)
                        S.act(lambda e, pb_=pb_: e.activation(out=th.ap, in_=pb_.ap, func=AF.Tanh, scale=0.5), reads=[pb_],
                              writes=[th])
                        S.dve(lambda e, i=i, j=j, pa_=pa_: e.scalar_tensor_tensor(
                            out=cbuf.ap[:, i, 15 + j * 512:15 + (j + 1) * 512], in0=th.ap, scalar=1.0, in1=pa_.ap,
                            op0=ALU.add, op1=ALU.mult), reads=[th, pa_], writes=[cbuf], partial=True)
                cv = S.sb("ccv", [128, 4, 512], F32, e1)
                sq = S.sb("csq", [128, 4, 512], BF16, e1)
                cvb = S.sb("ccvb", [128, 4, 512], BF16, e1)
                m2 = S.sb("cm2", [128, 512], F32, e1)
                rstd = S.sb("crs", [128, 512], F32, e1)
                dd2 = [S.sb("cdd%d" % i_, [128, 512], F32, e1) for i_ in range(2)]
                y22 = [S.sb("cy2%d" % i_, [128, 512], F32, e1) for i_ in range(2)]
                zst = [S.sb("czs%d" % i, [128, 4, 512], BF16, e1) for i in range(2)]
                for j in range(Sq // 512):
                    for i in range(4):
                        pc = P[i % 2]
                        S.mm(pc.ap, [(diag.ap[:, i, tp, :], cbuf.ap[:, i, j * 512 + tp:j * 512 + tp + 512]) for tp in range(31)],
                             [diag, cbuf], pc)
                        S.act(lambda e, i=i, pc=pc: e.activation(out=cv.ap[:, i, :], in_=pc.ap, func=AF.Identity,
                                                                 bias=cbias.ap[:, i, 0:1]), reads=[pc, cbias], writes=[cv],
                              partial=True)
                        S.act(lambda e, i=i: e.activation(out=sq.ap[:, i, :], in_=cv.ap[:, i, :], func=AF.Square),
                              reads=[cv], writes=[sq], partial=True)
                        S.dve(lambda e, i=i: e.tensor_copy(out=cvb.ap[:, i, :], in_=cv.ap[:, i, :]), reads=[cv], writes=[cvb],
                              partial=True)
                    S.mm(P[2].ap, [(onesB512.ap, cvb.ap[:, i, :]) for i in range(4)], [onesB512, cvb], P[2])
                    S.mm(P[3].ap, [(onesB512.ap, sq.ap[:, i, :]) for i in range(4)], [onesB512, sq], P[3])
                    S.act(lambda e: e.activation(out=m2.ap, in_=P[2].ap, func=AF.Square), reads=[P[2]], writes=[m2])
                    S.dve(lambda e: e.scalar_tensor_tensor(out=rstd.ap, in0=P[3].ap, scalar=EPS, in1=m2.ap, op0=ALU.add,
                                                           op1=ALU.subtract), reads=[P[3], m2], writes=[rstd])
                    S.act(lambda e: e.activation(out=rstd.ap, in_=rstd.ap, func=AF.Sqrt), reads=[rstd], writes=[rstd])
                    S.dve(lambda e: e.reciprocal(out=rstd.ap, in_=rstd.ap), reads=[rstd], writes=[rstd])
                    zs = zst[j % 2]
                    for i in range(4):
                        dd = dd2[i % 2]
                        y2 = y22[i % 2]
                        S.dve(lambda e, i=i: e.tensor_tensor(out=dd.ap, in0=cv.ap[:, i, :], in1=P[2].ap, op=ALU.subtract),
                              reads=[cv, P[2]], writes=[dd])
                        S.dve(lambda e: e.tensor_tensor(out=dd.ap, in0=dd.ap, in1=rstd.ap, op=ALU.mult), reads=[dd, rstd],
                              writes=[dd])
                        S.act(lambda e, i=i: e.activation(out=y2.ap, in_=dd.ap, func=AF.Identity, scale=lng.ap[:, i, 0:1],
                                                          bias=lnb.ap[:, i, 0:1]), reads=[dd, lng, lnb], writes=[y2])
                        S.act(lambda e: e.activation(out=dd.ap, in_=y2.ap, func=AF.Tanh), reads=[y2], writes=[dd])
                        S.dve(lambda e, i=i, zs=zs: e.scalar_tensor_tensor(out=zs.ap[:, i, :], in0=dd.ap, scalar=1.0, in1=y2.ap,
                                                                           op0=ALU.add, op1=ALU.mult), reads=[dd, y2],
                              writes=[zs], partial=True)
                    t0 = tok0[s] + j * 512
                    S.dma("pool", zT[0][:, t0:t0 + 512].rearrange("(c p) n -> p c n", p=128), zs.ap, reads=[zs], writes=[d_z[0]],
                          partial=True)
                S.barrier()

        def branch_attn(l, s, xnT, Sq):
            lam_init = 0.8 - 0.6 * math.exp(-0.3 * l)
            nkc = Sq // 128
            nqt = Sq // 512
            with ExitStack() as e1:
                lamr = S.sb("lamr", [128, 256], F32, e1)
                S.dma("sp", lamr.ap, sm["attn_lambda"][l:l + 1].rearrange("o a b -> o (a b)").partition_broadcast(128),
                      writes=[lamr])
                lj = S.sb("lamj", [128, 64], F32, e1)
                ls = S.sb("lams", [128, 2], F32, e1)
                neglam = S.sb("neglam", [128, 1], F32, e1)
                for t in range(2):
                    S.dve(lambda e, t=t: e.scalar_tensor_tensor(out=lj.ap, in0=lamr.ap[:, 128 * t:128 * t + 64], scalar=1.0,
                                                                in1=lamr.ap[:, 128 * t + 64:128 * t + 128], op0=ALU.mult,
                                                                op1=ALU.mult, accum_out=ls.ap[:, t:t + 1]),
                          reads=[lamr], writes=[lj, ls], partial=True)
                S.act(lambda e: e.activation(out=ls.ap, in_=ls.ap, func=AF.Exp), reads=[ls], writes=[ls])
                S.dve(lambda e: e.tensor_tensor(out=neglam.ap, in0=ls.ap[:, 1:2], in1=ls.ap[:, 0:1], op=ALU.subtract),
                      reads=[ls], writes=[neglam])
                S.dve(lambda e: e.tensor_scalar(out=neglam.ap, in0=neglam.ap, scalar1=-lam_init, scalar2=None, op0=ALU.add),
                      reads=[neglam], writes=[neglam])
                sgr = cols_from_rows(e1, "asg", sm["attn_subln_g"][l:l + 1, :], 1, 128, scale=(1.0 - lam_init))
                qT = S.sb("aqT", [128, Sq], BF16, e1)
                kT = S.sb("akT", [128, Sq], BF16, e1)
                wq = [S.sb("awq%d" % i, [128, 8, 128], BF16, e1) for i in range(3)]
                rc = [S.sb("arc%d" % i, [128, 512], F32, e1) for i in range(2)]
                rs = [S.sb("ars%d" % i, [128, 512], F32, e1) for i in range(2)]
                qraw2 = [S.sb("aqr%d" % a_, [128, 512], BF16, e1) for a_ in range(2)]
                t12 = [S.sb("at1%d" % a_, [128, 512], F32, e1) for a_ in range(2)]
                t22 = [S.sb("at2%d" % a_, [128, 512], F32, e1) for a_ in range(2)]
                pr = [[S.sb("apr%d%d" % (a, b), [128, 512], BF16, e1) for b in range(2)] for a in range(2)]
                o0 = S.sb("ao0", [128, 512], F32, e1)
                o1 = S.sb("ao1", [128, 512], F32, e1)
                rd = S.sb("ard", [128, 512], F32, e1)
                pacc = [S.sb("apacc%d" % c, [128, 512], F32, e1) for c in range(2)]
                paccb = [S.sb("apaccb%d" % c, [128, 512], BF16, e1) for c in range(2)]
                osq = S.sb("aosq", [128, 512], BF16, e1)
                zst = [S.sb("azs%d" % i, [128, 512], BF16, e1) for i in range(2)]
                it = 0
                vall = S.sb("avall", [128, nkc, 512], BF16, e1)
                wvall = S.sb("awvall", [128, 4, 8, 128], BF16, e1)
                for hh_ in range(4):
                    S.dma("sp", wvall.ap[:, hh_], wtile("w_in", l, 16 + hh_), reads=[d_wb[("w_in", l)]], writes=[wvall],
                          partial=(hh_ > 0))
                for m in range(nkc):
                    pv = P[2 + (m % 2)]
                    S.mm(pv.ap.rearrange("p (a n) -> p a n", a=4),
                         [(xnT.ap[:, k, m * 128:(m + 1) * 128], wvall.ap[:, :, k, :]) for k in range(8)], [wvall, xnT], pv)
                    S.act(lambda e, m=m, pv=pv: e.copy(out=vall.ap[:, m, :], in_=pv.ap), reads=[pv], writes=[vall], partial=True)
                for i in range(4):
                    for a, cb_ in enumerate((8, 12)):
                        S.dma("sp", wq[a].ap, wtile("w_in", l, cb_ + i), reads=[d_wb[("w_in", l)]], writes=[wq[a]])
                    for j in range(nqt):
                        S.dma("sp", rc[j % 2].ap, ropeC[:, j * 512:(j + 1) * 512], writes=[rc[j % 2]])
                        S.dma("sp", rs[j % 2].ap, ropeS[:, j * 512:(j + 1) * 512], writes=[rs[j % 2]])
                        for a, dstT in ((0, qT), (1, kT)):
                            pj, pm_ = P[4 * a], P[1 + 4 * a]
                            qraw, t1, t2 = qraw2[a], t12[a], t22[a]
                            S.mm(pj.ap, [(wq[a].ap[:, k, :], xnT.ap[:, k, j * 512:(j + 1) * 512]) for k in range(8)],
                                 [wq[a], xnT], pj)
                            S.act(lambda e, qraw=qraw, pj=pj: e.copy(out=qraw.ap, in_=pj.ap), reads=[pj], writes=[qraw])
                            S.mm(pm_.ap, [(permB.ap, qraw.ap)], [permB, qraw], pm_)
                            S.dve(lambda e, j=j, t1=t1, pj=pj: e.tensor_tensor(out=t1.ap, in0=pj.ap, in1=rc[j % 2].ap, op=ALU.mult),
                                  reads=[pj, rc[j % 2]], writes=[t1])
                            S.dve(lambda e, j=j, t2=t2, pm_=pm_: e.tensor_tensor(out=t2.ap, in0=pm_.ap, in1=rs[j % 2].ap, op=ALU.mult),
                                  reads=[pm_, rs[j % 2]], writes=[t2])
                            S.pool(lambda e, j=j, dstT=dstT, t1=t1, t2=t2: e.tensor_tensor(out=dstT.ap[:, j * 512:(j + 1) * 512], in0=t1.ap,
                                                                                           in1=t2.ap, op=ALU.add), reads=[t1, t2],
                                   writes=[dstT], partial=True)
                    for jq in range(nqt):
                        qs = slice(jq * 512, (jq + 1) * 512)
                        def emit_scores(kc):
                            ks = slice(kc * 128, (kc + 1) * 128)
                            pp = pr[kc % 2]
                            for c in range(2):
                                psc = P[c + 4 * (kc % 2)]
                                S.mm(psc.ap, [(kT.ap[c * 64:(c + 1) * 64, ks], qT.ap[c * 64:(c + 1) * 64, qs])], [kT, qT], psc)
                                S.act(lambda e, c=c, pp=pp, psc=psc: e.activation(out=pp[c].ap, in_=psc.ap, func=AF.Exp, scale=0.125),
                                      reads=[psc], writes=[pp[c]])

                        emit_scores(0)
                        for kc in range(nkc):
                            if kc + 1 < nkc:
                                emit_scores(kc + 1)
                            pp = pr[kc % 2]
                            for c in range(2):
                                S.pe(lambda e, c=c, kc=kc, pp=pp: e.matmul(P[2 + c].ap, lhsT=vall.ap[:, kc, i * 128:(i + 1) * 128],
                                                                           rhs=pp[c].ap, start=(kc == 0), stop=(kc == nkc - 1)),
                                     reads=[vall, pp[c]], writes=[P[2 + c]])
                                eng = S.dve if c == 0 else S.pool
                                if kc == 0:
                                    eng(lambda e, c=c, pp=pp: e.tensor_copy(out=pacc[c].ap, in_=pp[c].ap), reads=[pp[c]],
                                        writes=[pacc[c]])
                                else:
                                    eng(lambda e, c=c, pp=pp: e.tensor_tensor(out=pacc[c].ap, in0=pacc[c].ap, in1=pp[c].ap,
                                                                              op=ALU.add), reads=[pp[c], pacc[c]], writes=[pacc[c]])
                        for c in range(2):
                            S.act(lambda e, c=c: e.copy(out=paccb[c].ap, in_=pacc[c].ap), reads=[pacc[c]], writes=[paccb[c]])
                            S.mm(P[c].ap, [(onesB.ap, paccb[c].ap)], [onesB, paccb[c]], P[c])
                        S.dve(lambda e: e.reciprocal(out=rd.ap, in_=P[0].ap), reads=[P[0]], writes=[rd])
                        S.dve(lambda e: e.tensor_tensor(out=o0.ap, in0=P[2].ap, in1=rd.ap, op=ALU.mult), reads=[P[2], rd],
                              writes=[o0])
                        S.dve(lambda e: e.reciprocal(out=rd.ap, in_=P[1].ap), reads=[P[1]], writes=[rd])
                        S.dve(lambda e: e.tensor_tensor(out=o1.ap, in0=P[3].ap, in1=rd.ap, op=ALU.mult), reads=[P[3], rd],
                              writes=[o1])
                        S.dve(lambda e: e.scalar_tensor_tensor(out=o0.ap, in0=o1.ap, scalar=neglam.ap, in1=o0.ap, op0=ALU.mult,
                                                               op1=ALU.add), reads=[o1, neglam, o0], writes=[o0])
                        S.act(lambda e: e.activation(out=osq.ap, in_=o0.ap, func=AF.Square), reads=[o0], writes=[osq])
                        S.mm(P[0].ap, [(onesB128.ap, osq.ap)], [onesB128, osq], P[0])
                        S.act(lambda e: e.activation(out=rd.ap, in_=P[0].ap, func=AF.Sqrt, bias=epsc.ap[:, 0:1]),
                              reads=[P[0], epsc], writes=[rd])
                        S.dve(lambda e: e.reciprocal(out=rd.ap, in_=rd.ap), reads=[rd], writes=[rd])
                        S.dve(lambda e: e.tensor_tensor(out=o0.ap, in0=o0.ap, in1=rd.ap, op=ALU.mult), reads=[o0, rd],
                              writes=[o0])
                        zs = zst[it % 2]
                        it += 1
                        S.act(lambda e, zs=zs: e.activation(out=zs.ap, in_=o0.ap, func=AF.Identity, scale=sgr.ap[:, 0, 0:1]),
                              reads=[o0, sgr], writes=[zs])
                        t0 = tok0[s] + jq * 512
                        S.dma("pool", zT[1][i * 128:(i + 1) * 128, t0:t0 + 512], zs.ap, reads=[zs], writes=[d_z[1]], partial=True)
                S.barrier()

        def branch_hgrn(l, s, xnT, Sq):
            nt = Sq // 128
            nch = Sq // 64
            nq = Sq // 512
            with ExitStack() as e1:
                ngr = cols_from_rows(e1, "hng", sm["hg_norm_g"][l:l + 1, :], 1, 128, scale=0.5)
                q2 = S.sb("hq2", [128, Sq], F32, e1)
                dual = Sq <= 2048
                dsets = []
                for di in range(2 if dual else 1):
                    dsets.append((S.sb("hB2", [128, Sq], F32, e1), S.sb("hB34", [128, 2 * Sq], F32, e1),
                                  S.sb("hqt", [128, Sq], BF16, e1), S.sb("hkt", [128, Sq], BF16, e1),
                                  S.sb("hkk", [128, nt, 128], BF16, e1), S.sb("hAm", [128, 4, 64], BF16, e1),
                                  S.sb("hbnd", [128, 3, nch], F32, e1), S.sb("hdA", [128, nch], F32, e1),
                                  S.sb("hdB", [128, nch], F32, e1), S.sb("hdC", [128, nch], F32, e1)))
                vtok = S.sb("hvt", [128, nt, 512 if Sq <= 2048 else 128], BF16, e1)
                osum = S.sb("hos", [128, Sq], F32, e1)
                one1 = S.sb("hone", [128, 1], F32, e1)
                S.dve(lambda e: e.memset(one1.ap, 1.0), writes=[one1])
                wv = [S.sb("hw%d" % i, [128, 8, 128], BF16, e1) for i in range(4)]
                th2 = [S.sb("hth%d" % i_, [128, 512], F32, e1) for i_ in range(2)]
                t52 = [S.sb("ht5%d" % i_, [128, 512], F32, e1) for i_ in range(2)]
                t5b2 = [S.sb("ht5b%d" % i_, [128, 512], BF16, e1) for i_ in range(2)]
                th = th2[0]
                zst = [S.sb("hzs%d" % i, [128, 512], BF16, e1) for i in range(2)]
                wi = [0]

                def loadw(col):
                    w = wv[wi[0] % 4]
                    wi[0] += 1
                    S.dma("sp", w.ap, wtile("w_in", l, col // 128), reads=[d_wb[("w_in", l)]], writes=[w])
                    return w

                def proj_tile(w, j, pp):
                    S.mm(pp.ap, [(w.ap[:, k, :], xnT.ap[:, k, j * 512:(j + 1) * 512]) for k in range(8)], [w, xnT], pp)

                for i in range(4):
                    w = loadw(2560 + i * 128)
                    for j in range(nq):
                        pp = P[j % 2]
                        proj_tile(w, j, pp)
                        S.act(lambda e, pp=pp: e.activation(out=th.ap, in_=pp.ap, func=AF.Tanh, scale=0.5), reads=[pp], writes=[th])
                        S.dve(lambda e, j=j, pp=pp: e.scalar_tensor_tensor(out=q2.ap[:, j * 512:(j + 1) * 512], in0=th.ap,
                                                                           scalar=1.0, in1=pp.ap, op0=ALU.add, op1=ALU.mult),
                              reads=[th, pp], writes=[q2], partial=True)
                    if dual:
                        vtok_ap = vtok.ap[:, :, i * 128:(i + 1) * 128]
                        if i == 0:
                            wvall = S.sb("hwvall", [128, 4, 8, 128], BF16, e1)
                            for hh_ in range(4):
                                S.dma("sp", wvall.ap[:, hh_], wtile("w_in", l, 32 + hh_), reads=[d_wb[("w_in", l)]], writes=[wvall],
                                      partial=(hh_ > 0))
                            for m in range(nt):
                                pv = P[2 + (m % 2)]
                                S.mm(pv.ap.rearrange("p (a n) -> p a n", a=4),
                                     [(xnT.ap[:, k, m * 128:(m + 1) * 128], wvall.ap[:, :, k, :]) for k in range(8)], [wvall, xnT], pv)
                                S.act(lambda e, m=m, pv=pv: e.copy(out=vtok.ap[:, m, :], in_=pv.ap), reads=[pv], writes=[vtok],
                                      partial=True)
                    else:
                        vtok_ap = vtok.ap
                        w = loadw(4096 + i * 128)
                        for m in range(nt):
                            pv = P[2 + (m % 2)]
                            S.mm(pv.ap[:, 0:128], [(xnT.ap[:, k, m * 128:(m + 1) * 128], w.ap[:, k, :]) for k in range(8)], [w, xnT], pv)
                            S.act(lambda e, m=m, pv=pv: e.copy(out=vtok.ap[:, m, :], in_=pv.ap[:, 0:128]), reads=[pv], writes=[vtok],
                                  partial=True)
                    def dir_gen(d, B2, B34, qtl, ktl, ktok, Am, bnd, dA, dB, dC):
                        B3 = B34.ap[:, 0:Sq]
                        B4 = B34.ap[:, Sq:2 * Sq]
                        Uall = B34.ap.rearrange("p (c v) -> p c v", v=128)
                        Sbf = Buf("hSb_alias")
                        Sbf.ap = B2.ap.bitcast(BF16).rearrange("p (c v) -> p c v", v=128)
                        lbcol = lbs.ap[:, l, d, i:i + 1]
                        yield
                        omcol = omlb.ap[:, l, d, i:i + 1]
                        yield
                        w = loadw((3072 if d == 0 else 3584) + i * 128)
                        yield
                        for j in range(nq):
                            pp = P[j % 2]
                            proj_tile(w, j, pp)
                            S.act(lambda e, j=j, pp=pp: e.activation(out=B2.ap[:, j * 512:(j + 1) * 512], in_=pp.ap, func=AF.Exp,
                                                                     scale=-1.0), reads=[pp], writes=[B2], partial=True)
                            yield
                        yield
                        S.act(lambda e: e.activation(out=B3, in_=B2.ap, func=AF.Ln, bias=1.0), reads=[B2], writes=[B34])
                        yield
                        S.act(lambda e: e.activation(out=B4, in_=B2.ap, func=AF.Ln, bias=1.0, scale=lbcol), reads=[B2, lbs],
                              writes=[B34], partial=True)
                        yield
                        S.dve(lambda e: e.tensor_tensor(out=B4, in0=B4, in1=B3, op=ALU.subtract), reads=[B34], writes=[B34])
                        yield
                        S.act(lambda e: e.activation(out=B3, in_=B3, func=AF.Exp, scale=-1.0), reads=[B34], writes=[B34])
                        yield
                        S.dve(lambda e: e.scalar_tensor_tensor(out=B2.ap, in0=B2.ap, scalar=omcol, in1=B3, op0=ALU.mult,
                                                               op1=ALU.mult), reads=[B2, omlb, B34], writes=[B2])
                        yield
                        S.dve(lambda e: e.tensor_tensor_scan(out=B3, data0=one1.ap[:, 0:1].to_broadcast([128, Sq]), data1=B4,
                                                             initial=0.0, op0=ALU.mult, op1=ALU.add), reads=[B34, one1],
                              writes=[B34])
                        yield
                        if d == 1:
                            S.dve(lambda e: e.tensor_tensor(out=B3, in0=B3, in1=B4, op=ALU.subtract), reads=[B34], writes=[B34])
                        yield
                        G3 = B3.rearrange("p (c n) -> p c n", n=64)
                        yield
                        S.dve(lambda e: e.tensor_copy(out=bnd.ap[:, 0, :], in_=G3[:, :, 31]), reads=[B34], writes=[bnd])
                        yield
                        if d == 0:
                            S.dve(lambda e: e.tensor_copy(out=bnd.ap[:, 1, :], in_=G3[:, :, 63]), reads=[B34], writes=[bnd],
                                  partial=True)
                            S.dve(lambda e: e.memset(bnd.ap[:, 2, 0:1], 0.0), writes=[bnd], partial=True)
                            if nch > 1:
                                S.dve(lambda e: e.tensor_copy(out=bnd.ap[:, 2, 1:nch], in_=G3[:, 0:nch - 1, 63]), reads=[B34],
                                      writes=[bnd], partial=True)
                        else:
                            g3 = B4.rearrange("p (c n) -> p c n", n=64)
                            S.dve(lambda e: e.tensor_tensor(out=bnd.ap[:, 1, :], in0=G3[:, :, 63], in1=g3[:, :, 63], op=ALU.add),
                                  reads=[B34], writes=[bnd], partial=True)
                            S.dve(lambda e: e.tensor_copy(out=bnd.ap[:, 2, :], in_=G3[:, :, 0]), reads=[B34], writes=[bnd],
                                  partial=True)
                        yield
                        if d == 0:
                            prs = ((dA, 0, 2), (dB, 1, 0), (dC, 1, 2))
                        else:
                            prs = ((dA, 1, 0), (dB, 0, 2), (dC, 1, 2))
                        yield
                        for dst, a_, b_ in prs:
                            S.dve(lambda e, dst=dst, a_=a_, b_=b_: e.tensor_tensor(out=dst.ap, in0=bnd.ap[:, a_, :],
                                                                                   in1=bnd.ap[:, b_, :], op=ALU.subtract),
                                  reads=[bnd], writes=[dst])
                            S.act(lambda e, dst=dst: e.activation(out=dst.ap, in_=dst.ap, func=AF.Exp), reads=[dst], writes=[dst])
                            yield
                        yield
                        S.dve(lambda e: e.tensor_tensor(out=G3, in0=G3, in1=bnd.ap[:, 0, :].unsqueeze(2).to_broadcast([128, nch, 64]),
                                                        op=ALU.subtract), reads=[B34, bnd], writes=[B34])
                        yield
                        sgn = 1.0 if d == 0 else -1.0
                        yield
                        S.act(lambda e: e.activation(out=B4, in_=B3, func=AF.Exp, scale=sgn), reads=[B34], writes=[B34])
                        yield
                        S.dve(lambda e: e.scalar_tensor_tensor(out=qtl.ap, in0=q2.ap, scalar=0.5, in1=B4, op0=ALU.mult,
                                                               op1=ALU.mult), reads=[q2, B34], writes=[qtl])
                        yield
                        S.act(lambda e: e.activation(out=B4, in_=B3, func=AF.Exp, scale=-sgn), reads=[B34], writes=[B34])
                        yield
                        S.dve(lambda e: e.tensor_tensor(out=ktl.ap, in0=B2.ap, in1=B4, op=ALU.mult), reads=[B2, B34], writes=[ktl])
                        yield
                        for m in range(nt):
                            tb = T[m % 2]
                            S.pe(lambda e, m=m, tb=tb: e.transpose(out=tb.ap[:, 0:128], in_=ktl.ap[:, m * 128:(m + 1) * 128],
                                                                   identity=identB.ap), reads=[ktl, identB], writes=[tb])
                            S.act(lambda e, m=m, tb=tb: e.copy(out=ktok.ap[:, m, :], in_=tb.ap[:, 0:128]), reads=[tb],
                                  writes=[ktok], partial=True)
                            yield
                        yield
                        Uall4 = Uall.rearrange("p (t h) v -> p t h v", h=2)
                        yield
                        dB2 = dB.ap.rearrange("p (t h) -> p t h", h=2)
                        yield
                        for t4 in range(0, nt, 4):
                            for tt in range(4):
                                m = t4 + tt
                                for hh in range(2):
                                    pu = P[4 + hh]
                                    S.pe(lambda e, tt=tt, m=m, hh=hh, pu=pu: e.matmul(
                                        pu.ap[:, tt * 128:(tt + 1) * 128], lhsT=ktok.ap[hh * 64:(hh + 1) * 64, m, :],
                                        rhs=vtok_ap[hh * 64:(hh + 1) * 64, m, :], start=True, stop=True),
                                        reads=[ktok, vtok], writes=[pu], inc=(tt == 3))
                            for hh in range(2):
                                pu = P[4 + hh]
                                S.dve(lambda e, t4=t4, hh=hh, pu=pu: e.tensor_tensor(
                                    out=Uall4[:, t4:t4 + 4, hh, :], in0=pu.ap.rearrange("p (c v) -> p c v", v=128),
                                    in1=dB2[:, t4:t4 + 4, hh].unsqueeze(2).to_broadcast([128, 4, 128]), op=ALU.mult),
                                    reads=[pu, dB], writes=[B34], partial=True)
                            yield
                        yield
                        order = list(range(nch)) if d == 0 else list(range(nch - 1, -1, -1))
                        yield
                        for idx in range(1, nch):
                            c, cp = order[idx], order[idx - 1]
                            S.dve(lambda e, c=c, cp=cp: e.scalar_tensor_tensor(out=Uall[:, c, :], in0=Uall[:, cp, :],
                                                                               scalar=dC.ap[:, c:c + 1], in1=Uall[:, c, :],
                                                                               op0=ALU.mult, op1=ALU.add),
                                  reads=[B34, dC], writes=[B34])
                            yield
                        yield
                        first = order[0]
                        yield
                        S.dve(lambda e: e.memset(Sbf.ap[:, first, :], 0.0), writes=[B2])
                        yield
                        if nch > 1:
                            if d == 0:
                                S.dve(lambda e: e.tensor_tensor(out=Sbf.ap[:, 1:nch, :], in0=Uall[:, 0:nch - 1, :],
                                                                in1=dA.ap[:, 1:nch].unsqueeze(2).to_broadcast([128, nch - 1, 128]),
                                                                op=ALU.mult), reads=[B34, dA], writes=[B2], partial=True)
                            else:
                                S.dve(lambda e: e.tensor_tensor(out=Sbf.ap[:, 0:nch - 1, :], in0=Uall[:, 1:nch, :],
                                                                in1=dA.ap[:, 0:nch - 1].unsqueeze(2).to_broadcast([128, nch - 1, 128]),
                                                                op=ALU.mult), reads=[B34, dA], writes=[B2], partial=True)
                        yield
                        msk = maskF if d == 0 else maskBk
                        yield
                        for j in range(nq):
                            pa_ = P[j % 2]
                            po = P[2 + (j % 2)]
                            for mm_ in range(4):
                                m = j * 4 + mm_
                                for hh in range(2):
                                    c = 2 * m + hh
                                    S.pe(lambda e, c=c, hh=hh, mm_=mm_, pa_=pa_: e.matmul(
                                        pa_.ap[hh * 64:(hh + 1) * 64, mm_ * 64:(mm_ + 1) * 64], lhsT=ktl.ap[:, c * 64:(c + 1) * 64],
                                        rhs=qtl.ap[:, c * 64:(c + 1) * 64], start=True, stop=True), reads=[ktl, qtl], writes=[pa_],
                                        inc=(mm_ == 3 and hh == 1))
                            S.dve(lambda e, pa_=pa_: e.scalar_tensor_tensor(
                                out=Am.ap, in0=pa_.ap[:, 0:256].rearrange("p (m t) -> p m t", t=64), scalar=1e30,
                                in1=msk.ap.unsqueeze(1).to_broadcast([128, 4, 64]), op0=ALU.min, op1=ALU.mult),
                                reads=[pa_, msk], writes=[Am])
                            for mm_ in range(4):
                                m = j * 4 + mm_
                                for hh in range(2):
                                    c = 2 * m + hh
                                    cs = slice((mm_ * 2 + hh) * 64, (mm_ * 2 + hh + 1) * 64)
                                    S.pe(lambda e, m=m, hh=hh, mm_=mm_, cs=cs, po=po: e.matmul(
                                        po.ap[:, cs], lhsT=vtok_ap[hh * 64:(hh + 1) * 64, m, :], rhs=Am.ap[hh * 64:(hh + 1) * 64, mm_, :],
                                        start=True, stop=False), reads=[vtok, Am], writes=[po], inc=False)
                                    S.pe(lambda e, c=c, cs=cs, po=po: e.matmul(
                                        po.ap[:, cs], lhsT=Sbf.ap[:, c, :], rhs=qtl.ap[:, c * 64:(c + 1) * 64], start=False, stop=True),
                                        reads=[B2, qtl], writes=[po], inc=(mm_ == 3 and hh == 1))
                            S.dve(lambda e, j=j, po=po: e.tensor_tensor(out=osum.ap[:, j * 512:(j + 1) * 512],
                                                                        in0=osum.ap[:, j * 512:(j + 1) * 512], in1=po.ap,
                                                                        op=ALU.add), reads=[po, osum], writes=[osum])
                            yield
                    S.pool(lambda e: e.memset(osum.ap, 0.0), writes=[osum])
                    if dual:
                        g0 = dir_gen(0, *dsets[0])
                        g1 = dir_gen(1, *dsets[1])
                        alive = [g0, g1]
                        while alive:
                            for g_ in list(alive):
                                try:
                                    next(g_)
                                except StopIteration:
                                    alive.remove(g_)
                    else:
                        for d in range(2):
                            for _ in dir_gen(d, *dsets[0]):
                                pass
                    w = loadw(4608 + i * 128)
                    for j in range(nq):
                        js = slice(j * 512, (j + 1) * 512)
                        pp = P[j % 2]
                        th = th2[j % 2]
                        t5 = t52[j % 2]
                        t5b = t5b2[j % 2]
                        proj_tile(w, j, pp)
                        S.act(lambda e, pp=pp: e.activation(out=th.ap, in_=pp.ap, func=AF.Tanh, scale=0.5), reads=[pp], writes=[th])
                        S.dve(lambda e, pp=pp: e.scalar_tensor_tensor(out=th.ap, in0=th.ap, scalar=1.0, in1=pp.ap, op0=ALU.add,
                                                                      op1=ALU.mult), reads=[th, pp], writes=[th])
                        S.act(lambda e, js=js: e.activation(out=t5b.ap, in_=osum.ap[:, js], func=AF.Square), reads=[osum], writes=[t5b])
                        ps_ = P[2 + (j % 2)]
                        S.mm(ps_.ap, [(onesB128.ap, t5b.ap)], [onesB128, t5b], ps_)
                        S.act(lambda e, ps_=ps_: e.activation(out=t5.ap, in_=ps_.ap, func=AF.Sqrt, bias=epsc.ap[:, 0:1]),
                              reads=[ps_, epsc], writes=[t5])
                        S.dve(lambda e: e.reciprocal(out=t5.ap, in_=t5.ap), reads=[t5], writes=[t5])
                        S.dve(lambda e, js=js: e.tensor_tensor(out=t5.ap, in0=t5.ap, in1=osum.ap[:, js], op=ALU.mult),
                              reads=[t5, osum], writes=[t5])
                        S.dve(lambda e: e.tensor_tensor(out=t5.ap, in0=t5.ap, in1=th.ap, op=ALU.mult), reads=[t5, th], writes=[t5])
                        zs = zst[j % 2]
                        S.act(lambda e, zs=zs: e.activation(out=zs.ap, in_=t5.ap, func=AF.Identity, scale=ngr.ap[:, 0, 0:1]),
                              reads=[t5, ngr], writes=[zs])
                        t0 = tok0[s] + j * 512
                        S.dma("pool", zT[2][i * 128:(i + 1) * 128, t0:t0 + 512], zs.ap, reads=[zs], writes=[d_z[2]], partial=True)
                S.barrier()

        def merge(l, s, xnT, Sq, active):
            with ExitStack() as e1:
                ctxs = make_epi_ctx(e1, sm["norm_cross_g"][l:l + 1, :])
                wout = []
                for b, key in enumerate(("w_conv_out", "w_attn_out", "w_hg_out")):
                    wt = S.sb("mwo%d" % b, [128, 4, D], BF16, e1)
                    S.dma("sp", wt.ap, wb[key][l].rearrange("(c p) n -> p c n", p=128), reads=[d_wb[(key, l)]], writes=[wt])
                    wout.append(wt)
                wo = S.sb("mwo", [128, 8, D], BF16, e1)
                S.dma("sp", wo.ap, wb["w_o"][l].rearrange("(c p) n -> p c n", p=128), reads=[d_wb[("w_o", l)]], writes=[wo])
                resident = Sq <= 2048
                if resident:
                    wgall = S.sb("mwgall", [128, 8, 3, 8, 128], BF16, e1)
                    for c in range(8):
                        for b in active:
                            S.dma("sp", wgall.ap[:, c, b], wtile("w_in", l, 40 + b * 8 + c), reads=[d_wb[("w_in", l)]],
                                  writes=[wgall], partial=True)
                    wg = None
                else:
                    wg = [S.sb("mwg%d" % i, [128, 3, 8, 128], BF16, e1) for i in range(2)]
                zt = [S.sb("mz%d" % b, [128, 4, 512], BF16, e1) for b in range(3)]
                thb = [S.sb("mth%d" % b, [128, 512], F32, e1) for b in range(3)]
                mb = [S.sb("mmb%d" % b, [128, 512], F32, e1) for b in range(3)]
                mg = S.sb("mmg", [128, 8, 512], BF16, e1)
                stage = S.sb("mst", [128, 8, 512], BF16, e1)
                for g in range(Sq // 512):
                    t0 = tok0[s] + g * 512
                    gs = slice(g * 512, (g + 1) * 512)
                    for b in active:
                        S.dma("sp", zt[b].ap, zT[b][:, t0:t0 + 512].rearrange("(c p) n -> p c n", p=128), reads=[d_z[b]],
                              writes=[zt[b]])
                    for c in range(8):
                        if resident:
                            w = Buf("wgview")
                            w = wgall
                            wap = wgall.ap[:, c]
                        else:
                            w = wg[c % 2]
                            wap = w.ap
                            for b in active:
                                col = 5120 + b * 1024 + c * 128
                                S.dma("sp", w.ap[:, b], wtile("w_in", l, col // 128), reads=[d_wb[("w_in", l)]], writes=[w],
                                      partial=True)
                        for b in active:
                            py, pg = P[2 * b], P[2 * b + 1]
                            S.mm(py.ap, [(wout[b].ap[:, k, c * 128:(c + 1) * 128], zt[b].ap[:, k, :]) for k in range(4)],
                                 [wout[b], zt[b]], py)
                            S.mm(pg.ap, [(wap[:, b, k, :], xnT.ap[:, k, gs]) for k in range(8)], [w, xnT], pg)
                            th = thb[b]
                            S.act(lambda e, pg=pg, th=th: e.activation(out=th.ap, in_=pg.ap, func=AF.Tanh, scale=0.5), reads=[pg],
                                  writes=[th])
                            S.dve(lambda e, b=b, py=py, th=th: e.scalar_tensor_tensor(out=mb[b].ap, in0=th.ap, scalar=1.0, in1=py.ap,
                                                                                      op0=ALU.add, op1=ALU.mult), reads=[th, py],
                                  writes=[mb[b]])
                        acc = mb[active[0]]
                        for b in active[1:-1]:
                            S.dve(lambda e, b=b, acc=acc: e.tensor_tensor(out=acc.ap, in0=acc.ap, in1=mb[b].ap, op=ALU.add),
                                  reads=[acc, mb[b]], writes=[acc])
                        if len(active) > 1:
                            S.dve(lambda e, c=c, acc=acc: e.tensor_tensor(out=mg.ap[:, c, :], in0=acc.ap, in1=mb[active[-1]].ap,
                                                                          op=ALU.add), reads=[acc, mb[active[-1]]], writes=[mg],
                                  partial=True)
                        else:
                            S.dve(lambda e, c=c, acc=acc: e.tensor_copy(out=mg.ap[:, c, :], in_=acc.ap), reads=[acc], writes=[mg],
                                  partial=True)
                    for m in range(4):
                        for hh in range(2):
                            S.mm(P[hh].ap, [(mg.ap[:, k, m * 128:(m + 1) * 128], wo.ap[:, k, hh * 512:(hh + 1) * 512])
                                            for k in range(8)], [mg, wo], P[hh])
                        residual_epilogue(ctxs[m % 2], t0 + m * 128, P[0], P[1], False, stage, m * 128, scale=0.5)
                    c0 = hcol0[s] + g * 512
                    S.dma("pool", hT[1][:, c0:c0 + 512].rearrange("(c p) n -> p c n", p=128), stage.ap, reads=[stage],
                          writes=[d_h[1]], partial=True)
                S.barrier()

        def phase_mix(l):
            for s in range(nseq):
                Sq = seq_lens[s]
                with ExitStack() as e0:
                    xnT = S.sb("xnT", [128, 8, Sq], BF16, e0)
                    S.dma("sp", xnT.ap, hT[0][:, hcol0[s]:hcol0[s] + Sq].rearrange("(c p) n -> p c n", p=128), reads=[d_h[0]],
                          writes=[xnT])
                    active = []
                    if "conv" in flags:
                        branch_conv(l, s, xnT, Sq)
                        active.append(0)
                    if "attn" in flags:
                        branch_attn(l, s, xnT, Sq)
                        active.append(1)
                    if "hgrn" in flags:
                        branch_hgrn(l, s, xnT, Sq)
                        active.append(2)
                    merge(l, s, xnT, Sq, active)
                    S.barrier()

        MIX = phase_mix
        CROSS = phase_cross
        cast_layer(0)
        S.barrier()
        phase_prologue()
        for l in range(1, L):
            cast_layer(l)
        for l in range(L):
            if MIX is not None and any(f in flags for f in ("conv", "attn", "hgrn")):
                MIX(l)
            else:
                phase_copy_h(0, 1)
            if CROSS is not None and "cross" in flags:
                CROSS(l)
            else:
                phase_copy_h(1, 2)
            phase_ffn(l)
        S.barrier()
        print("instructions emitted:", S.ninst)
    return nc


def rope_tables(smax):
    inv = 1.0 / (500000.0 ** (np.arange(0, 16, 2, dtype=np.float32) / 16.0))
    ang = np.arange(smax, dtype=np.float32)[None, :] * inv[:, None].astype(np.float32)
    C = np.ones((128, smax), np.float32)
    Sg = np.zeros((128, smax), np.float32)
    for blk in (0, 64):
        C[blk:blk + 8] = np.cos(ang)
        C[blk + 8:blk + 16] = np.cos(ang)
        Sg[blk:blk + 8] = -np.sin(ang)
        Sg[blk + 8:blk + 16] = np.sin(ang)
    Pm = np.zeros((128, 128), np.float32)
    for blk in (0, 64):
        for d in range(8):
            Pm[blk + d + 8, blk + d] = 1.0
            Pm[blk + d, blk + d + 8] = 1.0
    return C, Sg, Pm


def make_in_maps(seq_assign, xs_list, mems_list, params, smax):
    C, Sg, Pm = rope_tables(smax)
    tiled = {}
    for k in ("w_in", "w_up"):
        w = np.asarray(params[k], dtype=np.float32)
        Lw, Kw, Nw = w.shape
        tiled[k] = np.ascontiguousarray(w.reshape(Lw, 8, 128, Nw // 128, 128).transpose(0, 3, 2, 1, 4)).reshape(
            Lw, Nw // 128, 128, 1024)
    maps = []
    for c in range(len(seq_assign)):
        m = {"x": np.ascontiguousarray(np.concatenate([xs_list[i] for i in seq_assign[c]], 0)),
             "mem": np.ascontiguousarray(np.concatenate([mems_list[i] for i in seq_assign[c]], 0)),
             "ropeC": C, "ropeS": Sg, "permM": Pm}
        for k in list(WSHAPES) + list(SMALL) + ["hg_lb_param"]:
            m[k] = tiled.get(k) if k in tiled else np.ascontiguousarray(params[k], dtype=np.float32)
        m["final_norm_g"] = np.ascontiguousarray(params["final_norm_g"], dtype=np.float32).reshape(1, D)
        maps.append(m)
    return maps


def kernel(**inputs):
    xp = np.asarray(inputs["x_prompt"], np.float32)
    xsm = np.asarray(inputs["x_sample"], np.float32)
    mp = np.asarray(inputs["mem_prompt"], np.float32)
    ms = np.asarray(inputs["mem_sample"], np.float32)
    depth = inputs["w_in"].shape[0]
    nb, sp = xp.shape[0], xp.shape[1]
    nsb, ssm = xsm.shape[0], xsm.shape[1]
    seqs = [xp[i] for i in range(nb)] + [xsm[i] for i in range(nsb)]
    mems = [mp[i] for i in range(nb)] + [ms[i] for i in range(nsb)]
    n = 8
    per = nb // n
    assign = [[c * per + i for i in range(per)] + [nb + (c % nsb)] for c in range(n)]
    seq_lens = [sp] * per + [ssm]
    nc = build(seq_lens, depth)
    maps = make_in_maps(assign, seqs, mems, inputs, max(sp, ssm))
    res = run_bass_kernel_spmd(nc, maps, core_ids=list(range(n)))
    yp = np.zeros_like(xp)
    ysm = np.zeros_like(xsm)
    for c in range(n):
        y = res.results[c]["y"]
        o = 0
        for i in assign[c]:
            ln = seqs[i].shape[0]
            if i < nb:
                yp[i] = y[o:o + ln]
            elif c < nsb:
                ysm[i - nb] = y[o:o + ln]
            o += ln
    return (yp, ysm)
```

```python
import math
import numpy as np
import concourse.bass as bass
import concourse.mybir as mybir
from concourse.bass_utils import run_bass_kernel_spmd
from contextlib import ExitStack

F32 = mybir.dt.float32
BF16 = mybir.dt.bfloat16
AF = mybir.ActivationFunctionType
ALU = mybir.AluOpType

D = 1024
MEM = 256
DFF = 2816
EPS = 1e-6
WSHAPES = {"w_in": (1024, 8192), "w_conv_out": (512, 1024), "w_attn_out": (512, 1024), "w_hg_out": (512, 1024),
           "w_o": (1024, 1024), "w_cq": (1024, 1024), "w_ckv": (1024, 2048), "w_co": (1024, 1024),
           "w_up": (1024, 5632), "w_down": (2816, 1024)}
SMALL = {"norm_mix_g": (1024,), "conv_dw_w": (31, 512), "conv_dw_b": (512,), "conv_ln_g": (512,), "conv_ln_b": (512,),
         "attn_lambda": (4, 64), "attn_subln_g": (128,), "hg_norm_g": (128,), "norm_cross_g": (1024,),
         "norm_mem_g": (1024,), "norm_ffn_g": (1024,), "ffn_dw_w": (3, 5632), "ffn_dw_b": (5632,)}


class Buf:
    __slots__ = ("w", "r", "name", "ap", "psum")

    def __init__(self, name="", ap=None, psum=False):
        self.w = []
        self.r = []
        self.name = name
        self.ap = ap
        self.psum = psum


class Sched:
    NDMA = 24

    def __init__(self, nc, es):
        self.nc = nc
        self.es = es
        self.eng = {"pe": nc.tensor, "act": nc.scalar, "dve": nc.vector, "pool": nc.gpsimd, "sp": nc.sync}
        self.sem = {}
        self.cnt = {}
        for k in self.eng:
            self.sem[k] = es.enter_context(nc.semaphore("s_" + k))
            self.cnt[k] = 0
        for q in ("sp", "pool", "act"):
            for i in range(self.NDMA):
                self.sem[("d" + q, i)] = es.enter_context(nc.semaphore("d%s%d" % (q, i)))
                self.cnt[("d" + q, i)] = 0
        self.dnext = {"sp": 0, "pool": 0, "act": 0}
        self.seen = {k: {} for k in self.eng}
        self.ninst = 0

    def sb(self, name, shape, dt, es=None):
        self.uid = getattr(self, "uid", 0) + 1
        name = "%s_%d" % (name, self.uid)
        t = (es or self.es).enter_context(self.nc.sbuf_tensor(name, shape, dt))
        return Buf(name, t.ap())

    def ps(self, name, shape, dt, es=None):
        t = (es or self.es).enter_context(self.nc.psum_tensor(name, shape, dt))
        return Buf(name, t.ap(), psum=True)

    def _wait(self, e, tok):
        key, val = tok
        if key == e and (e == "pe" or val > self.cnt[e]):
            return
        if self.seen[e].get(key, 0) >= val:
            return
        self.eng[e].wait_ge(self.sem[key], val)
        self.seen[e][key] = val
        self.ninst += 1

    def _deps(self, e, reads, writes):
        for b in reads:
            for tok in b.w:
                self._wait(e, tok)
            if b.psum:
                for tok in b.r:
                    if tok[0] != e:
                        self._wait(e, tok)
        for b in writes:
            for tok in b.w:
                self._wait(e, tok)
            for tok in b.r:
                self._wait(e, tok)

    def _commit(self, tok, reads, writes, partial):
        for b in writes:
            if partial:
                b.w = [t for t in b.w if t[0] != tok[0]]
                b.w.append(tok)
            else:
                b.w = [tok]
                b.r = []
        for b in reads:
            b.r = [t for t in b.r if t[0] != tok[0]]
            b.r.append(tok)

    def op(self, e, fn, reads=(), writes=(), inc=True, partial=False):
        self._deps(e, reads, writes)
        ins = fn(self.eng[e])
        self.ninst += 1
        if inc:
            self.cnt[e] += 1
            ins.then_inc(self.sem[e], 1)
            tok = (e, self.cnt[e])
        else:
            tok = (e, self.cnt[e] + 1)
        self._commit(tok, reads, writes, partial)
        return ins

    def pe(self, fn, **kw): return self.op("pe", fn, **kw)
    def act(self, fn, **kw): return self.op("act", fn, **kw)
    def dve(self, fn, **kw): return self.op("dve", fn, **kw)
    def pool(self, fn, **kw): return self.op("pool", fn, **kw)

    def dma(self, q, out, in_, reads=(), writes=(), partial=False, **kw):
        i = self.dnext[q]
        self.dnext[q] = (i + 1) % self.NDMA
        key = ("d" + q, i)
        if self.cnt[key] > 0:
            self._wait(q, (key, self.cnt[key]))
        self._deps(q, reads, writes)
        ins = self.eng[q].dma_start(out=out, in_=in_, **kw)
        self.ninst += 1
        self.cnt[key] += 16
        ins.then_inc(self.sem[key], 16)
        self._commit((key, self.cnt[key]), reads, writes, partial)

    def barrier(self):
        toks = [(k, v) for k, v in self.cnt.items() if v > 0]
        for e in self.eng:
            for tok in toks:
                if tok[0] != e:
                    self._wait(e, tok)

    def mm(self, out_ap, pairs, reads, out_buf):
        n = len(pairs)
        for i, (l, r) in enumerate(pairs):
            self.pe(lambda e, l=l, r=r, i=i: e.matmul(out_ap, lhsT=l, rhs=r, start=(i == 0), stop=(i == n - 1)),
                    reads=reads, writes=[out_buf], inc=(i == n - 1))


def build(seq_lens, depth, flags=("conv", "attn", "hgrn", "cross", "ffn")):
    nc = bass.Bass("TRN2", target_bir_lowering=False)
    nseq = len(seq_lens)
    NT = sum(seq_lens)
    SMAX = max(seq_lens)
    tok0 = [sum(seq_lens[:i]) for i in range(nseq)]
    hcol0 = [sum(s + 2 for s in seq_lens[:i]) + 1 for i in range(nseq)]
    HW = sum(s + 2 for s in seq_lens)
    L = depth

    def din(name, shape, dt=F32):
        return nc.dram_tensor(name, list(shape), dt, kind="ExternalInput").ap()

    def dscr(name, shape, dt):
        return nc.dram_tensor(name, list(shape), dt, kind="Internal").ap()

    x_in = din("x", [NT, D])
    mem_in = din("mem", [nseq * MEM, D])
    y_out = nc.dram_tensor("y", [NT, D], F32, kind="ExternalOutput").ap()
    TILED = {"w_in": 64, "w_up": 44}
    wshape = {k: ((L, TILED[k], 128, 1024) if k in TILED else (L,) + v) for k, v in WSHAPES.items()}
    wf = {k: din(k, wshape[k]) for k in WSHAPES}
    sm = {k: din(k, (L,) + v) for k, v in SMALL.items()}
    lb_in = din("hg_lb_param", (L, 2, 512))
    fin_g = din("final_norm_g", (1, D))
    ropeC = din("ropeC", (128, SMAX))
    ropeS = din("ropeS", (128, SMAX))
    permM = din("permM", (128, 128))
    wb = {k: dscr(k + "_b", wshape[k], BF16) for k in WSHAPES}

    def wtile(key, l, c):
        return wb[key][l, c].rearrange("p (k n) -> p k n", k=8)
    xs = dscr("xs", [NT, D], F32)
    hT = [dscr("hT%d" % i, [D, HW], BF16) for i in range(3)]
    zT = [dscr("zT%d" % i, [512, NT], BF16) for i in range(3)]
    d_wb = {(k, l): Buf() for k in WSHAPES for l in range(L)}
    d_xs = Buf("xs")
    d_h = [Buf("h%d" % i) for i in range(3)]
    d_z = [Buf("z%d" % i) for i in range(3)]
    d_y = Buf("y")

    with ExitStack() as es:
        S = Sched(nc, es)
        identB = S.sb("identB", [128, 128], BF16)
        identF = S.sb("identF", [128, 128], F32)
        onesB = S.sb("onesB", [128, 128], BF16)
        onesF = S.sb("onesF", [128, 128], F32)
        onesF512 = S.sb("onesF512", [128, 128], F32)
        onesF1 = S.sb("onesF1", [128, 128], F32)
        onesB128 = S.sb("onesB128", [128, 128], BF16)
        onesB512 = S.sb("onesB512", [128, 128], BF16)
        nhalf = S.sb("nhalf", [128, 8], F32)
        epsc = S.sb("epsc", [128, 1], F32)
        permB = S.sb("permB", [128, 128], BF16)
        maskF = S.sb("maskF", [128, 64], F32)
        maskBk = S.sb("maskBk", [128, 64], F32)
        zrow = S.sb("zrow", [128, 16], BF16)
        lbs = S.sb("lbs", [128, L, 2, 4], F32)
        omlb = S.sb("omlb", [128, L, 2, 4], F32)
        P = [S.ps("P%d" % i, [128, 512], F32) for i in range(6)]
        T = [S.ps("T%d" % i, [128, 1024], BF16) for i in range(2)]

        for b, v in ((identB, 1.0), (identF, 1.0), (onesB, 1.0), (onesF, 1.0 / 128), (onesF512, 1.0 / 512), (onesF1, 1.0), (onesB128, 1.0 / 128), (onesB512, 1.0 / 512),
                     (nhalf, -0.5), (epsc, EPS), (maskF, 1.0), (maskBk, 1.0), (zrow, 0.0)):
            S.pool(lambda e, b=b, v=v: e.memset(b.ap, v), writes=[b])
        for b in (identB, identF):
            S.pool(lambda e, b=b: e.affine_select(out=b.ap, in_=b.ap, pattern=[[-1, 128]], compare_op=ALU.is_equal,
                                                  fill=0.0, base=0, channel_multiplier=1), reads=[b], writes=[b])
        for h in range(2):
            S.pool(lambda e, h=h: e.affine_select(out=maskF.ap[h * 64:(h + 1) * 64, :], in_=maskF.ap[h * 64:(h + 1) * 64, :],
                                                  pattern=[[1, 64]], compare_op=ALU.is_ge, fill=0.0, base=0,
                                                  channel_multiplier=-1), reads=[maskF], writes=[maskF])
            S.pool(lambda e, h=h: e.affine_select(out=maskBk.ap[h * 64:(h + 1) * 64, :], in_=maskBk.ap[h * 64:(h + 1) * 64, :],
                                                  pattern=[[-1, 64]], compare_op=ALU.is_ge, fill=0.0, base=0,
                                                  channel_multiplier=1), reads=[maskBk], writes=[maskBk])
        S.dma("pool", permB.ap, permM, writes=[permB])
        for i in range(3):
            for s in range(nseq):
                for c in (hcol0[s] - 1, hcol0[s] + seq_lens[s]):
                    S.dma("pool", hT[i][:, c:c + 1].rearrange("(c p) n -> p c n", p=128),
                          zrow.ap[:, 0:8].rearrange("p (c n) -> p c n", n=1),
                          reads=[zrow], writes=[d_h[i]], partial=True, allow_slow_non_contiguous=True)
        for l in range(L):
            for k, (r, c) in WSHAPES.items():
                if k in TILED:
                    for c0 in range(0, TILED[k], 8):
                        c1 = min(TILED[k], c0 + 8)
                        S.dma("pool", wb[k][l, c0:c1], wf[k][l, c0:c1], writes=[d_wb[(k, l)]], partial=True)
                    continue
                step = 2048
                for c0 in range(0, c, step):
                    c1 = min(c, c0 + step)
                    S.dma("pool", wb[k][l, :, c0:c1], wf[k][l, :, c0:c1], writes=[d_wb[(k, l)]], partial=True)

        def cols_from_rows(es_, name, src2d, R, N, scale=None):
            nchk = N // 128
            out = S.sb(name, [128, nchk, R], F32, es_)
            er = ExitStack()
            rows = S.sb(name + "_r", [R, N], F32, er)
            S.dma("sp", rows.ap, src2d, writes=[rows])
            for c in range(nchk):
                S.pe(lambda e, c=c: e.matmul(P[5].ap[:, 0:R], lhsT=rows.ap[0:R, c * 128:(c + 1) * 128],
                                            rhs=identF.ap[0:R, 0:R], start=True, stop=True),
                     reads=[rows, identF], writes=[P[5]])
                if scale is None:
                    S.dve(lambda e, c=c: e.tensor_copy(out=out.ap[:, c, :], in_=P[5].ap[:, 0:R]), reads=[P[5]],
                          writes=[out], partial=True)
                else:
                    S.dve(lambda e, c=c: e.tensor_scalar(out=out.ap[:, c, :], in0=P[5].ap[:, 0:R], scalar1=scale,
                                                         scalar2=None, op0=ALU.mult), reads=[P[5]], writes=[out],
                          partial=True)
            S.barrier()
            er.close()
            return out

        with ExitStack() as e1:
            lbc = cols_from_rows(e1, "lbc", lb_in.rearrange("l t n -> (l t) n"), 2 * L, 512)
            ex = S.sb("lb_ex", [128, 4, 2 * L], F32, e1)
            sm_ = S.sb("lb_sm", [128, 4, 2], F32, e1)
            S.act(lambda e: e.activation(out=ex.ap, in_=lbc.ap, func=AF.Exp), reads=[lbc], writes=[ex])
            exv = ex.ap.rearrange("p c (l t) -> p c l t", t=2)
            S.dve(lambda e: e.tensor_copy(out=sm_.ap, in_=exv[:, :, 0, :]), reads=[ex], writes=[sm_])
            for l in range(1, L):
                S.dve(lambda e, l=l: e.tensor_tensor(out=sm_.ap, in0=sm_.ap, in1=exv[:, :, l, :], op=ALU.add),
                      reads=[ex, sm_], writes=[sm_])
            S.dve(lambda e: e.reciprocal(out=sm_.ap, in_=sm_.ap), reads=[sm_], writes=[sm_])
            lbv = lbs.ap.rearrange("p l t c -> p c l t")
            S.dve(lambda e: e.memset(lbv[:, :, 0, :], 0.0), writes=[lbs])
            for l in range(1, L):
                S.dve(lambda e, l=l: e.tensor_tensor(out=lbv[:, :, l, :], in0=exv[:, :, l, :], in1=sm_.ap, op=ALU.mult),
                      reads=[ex, sm_, lbs], writes=[lbs])
                if l > 1:
                    S.dve(lambda e, l=l: e.tensor_tensor(out=lbv[:, :, l, :], in0=lbv[:, :, l, :], in1=lbv[:, :, l - 1, :],
                                                         op=ALU.add), reads=[lbs], writes=[lbs])
            S.dve(lambda e: e.tensor_scalar(out=omlb.ap, in0=lbs.ap, scalar1=-1.0, scalar2=1.0, op0=ALU.mult, op1=ALU.add),
                  reads=[lbs], writes=[omlb])
            S.barrier()

        def norm_T(xt, gbc, stage, col, tmp):
            junk, ss, xn = tmp
            S.act(lambda e: e.activation(out=junk.ap, in_=xt.ap, func=AF.Square, accum_out=ss.ap), reads=[xt],
                  writes=[junk, ss])
            S.act(lambda e: e.activation(out=ss.ap, in_=ss.ap, func=AF.Sqrt, scale=1.0 / D, bias=epsc.ap[:, 0:1]),
                  reads=[ss, epsc], writes=[ss])
            S.dve(lambda e: e.reciprocal(out=ss.ap, in_=ss.ap), reads=[ss], writes=[ss])
            S.dve(lambda e: e.scalar_tensor_tensor(out=xn.ap, in0=xt.ap, scalar=ss.ap, in1=gbc.ap, op0=ALU.mult,
                                                   op1=ALU.mult), reads=[xt, ss, gbc], writes=[xn])
            return xn

        def transpose_to(xn, stage, col, tb):
            for c in range(8):
                S.pe(lambda e, c=c: e.transpose(out=tb.ap[:, c * 128:(c + 1) * 128], in_=xn.ap[:, c * 128:(c + 1) * 128],
                                                identity=identB.ap), reads=[xn, identB], writes=[tb], inc=(c == 7))
            S.act(lambda e: e.copy(out=stage.ap[:, :, col:col + 128], in_=tb.ap.rearrange("p (c n) -> p c n", c=8)),
                  reads=[tb], writes=[stage], partial=True)

        def load_gbc(es_, name, src_row):
            g = S.sb(name, [128, D], F32, es_)
            S.dma("sp", g.ap, src_row.partition_broadcast(128), writes=[g])
            return g

        def load_w(dst, key, l, c0, c1, r0=0, r1=None):
            r1 = WSHAPES[key][0] if r1 is None else r1
            S.dma("sp", dst, wb[key][l, r0:r1, c0:c1].rearrange("(c p) n -> p c n", p=128), reads=[d_wb[(key, l)]],
                  writes=[dst_b[0]])

        dst_b = [None]

        def ldw(dstbuf, dst_ap, key, l, c0, c1, r0=0, r1=None):
            dst_b[0] = dstbuf
            load_w(dst_ap, key, l, c0, c1, r0, r1)

        def phase_prologue():
            with ExitStack() as e1:
                gbc = load_gbc(e1, "gbc", sm["norm_mix_g"][0:1, :])
                xt = [S.sb("pxt%d" % i, [128, D], F32, e1) for i in range(2)]
                tmp = [(S.sb("pj%d" % i, [128, D], F32, e1), S.sb("pss%d" % i, [128, 1], F32, e1),
                        S.sb("pxn%d" % i, [128, D], BF16, e1)) for i in range(2)]
                stage = [S.sb("pst%d" % i, [128, 8, 512], BF16, e1) for i in range(2)]
                it = 0
                for s in range(nseq):
                    for j in range(seq_lens[s] // 512):
                        st = stage[j % 2]
                        for m in range(4):
                            t0 = tok0[s] + j * 512 + m * 128
                            b = it % 2
                            it += 1
                            S.dma("sp", xt[b].ap, x_in[t0:t0 + 128, :], writes=[xt[b]])
                            S.dma("pool", xs[t0:t0 + 128, :], xt[b].ap, reads=[xt[b]], writes=[d_xs], partial=True)
                            xn = norm_T(xt[b], gbc, st, m * 128, tmp[b])
                            transpose_to(xn, st, m * 128, T[b])
                        c0 = hcol0[s] + j * 512
                        S.dma("pool", hT[0][:, c0:c0 + 512].rearrange("(c p) n -> p c n", p=128), st.ap, reads=[st],
                              writes=[d_h[0]], partial=True)
                S.barrier()

        def residual_epilogue(ctx, t0, pa, pb, last_layer_final, st, col, scale=None):
            xt, gbc, tmp, tb, yt = ctx
            S.dma("sp", xt.ap, xs[t0:t0 + 128, :], reads=[d_xs], writes=[xt])
            for hh_, pp_ in ((0, pa), (1, pb)):
                sl_ = slice(hh_ * 512, (hh_ + 1) * 512)
                if scale is None:
                    S.dve(lambda e, sl_=sl_, pp_=pp_: e.tensor_tensor(out=xt.ap[:, sl_], in0=xt.ap[:, sl_], in1=pp_.ap, op=ALU.add),
                          reads=[xt, pp_], writes=[xt])
                else:
                    S.dve(lambda e, sl_=sl_, pp_=pp_: e.scalar_tensor_tensor(out=xt.ap[:, sl_], in0=pp_.ap, scalar=scale,
                                                                             in1=xt.ap[:, sl_], op0=ALU.mult, op1=ALU.add),
                          reads=[xt, pp_], writes=[xt])
            if not last_layer_final:
                S.dma("pool", xs[t0:t0 + 128, :], xt.ap, reads=[xt], writes=[d_xs], partial=True)
                xn = norm_T(xt, gbc, st, col, tmp)
                transpose_to(xn, st, col, tb)
            else:
                junk, ss, xn = tmp
                S.act(lambda e: e.activation(out=junk.ap, in_=xt.ap, func=AF.Square, accum_out=ss.ap), reads=[xt],
                      writes=[junk, ss])
                S.act(lambda e: e.activation(out=ss.ap, in_=ss.ap, func=AF.Sqrt, scale=1.0 / D, bias=epsc.ap[:, 0:1]),
                      reads=[ss, epsc], writes=[ss])
                S.dve(lambda e: e.reciprocal(out=ss.ap, in_=ss.ap), reads=[ss], writes=[ss])
                S.dve(lambda e: e.scalar_tensor_tensor(out=yt.ap, in0=xt.ap, scalar=ss.ap, in1=gbc.ap, op0=ALU.mult,
                                                       op1=ALU.mult), reads=[xt, ss, gbc], writes=[yt])
                S.dma("pool", y_out[t0:t0 + 128, :], yt.ap, reads=[yt], writes=[d_y], partial=True)

        def make_epi_ctx(e1, gsrc, n=2):
            gbc = load_gbc(e1, "egbc", gsrc)
            ctxs = []
            for i in range(n):
                junk_ = S.sb("ej%d" % i, [128, D], F32, e1)
                ctxs.append((S.sb("ext%d" % i, [128, D], F32, e1), gbc,
                             (junk_, S.sb("ess%d" % i, [128, 1], F32, e1),
                              S.sb("exn%d" % i, [128, D], BF16, e1)), T[i], junk_))
            return ctxs

        def phase_ffn(l):
            last = (l == L - 1)
            SEG = 1024 if all(sl % 1024 == 0 for sl in seq_lens) else 512
            cranges = [(c0, min(c0 + 512, SEG + 2)) for c0 in range(0, SEG + 2, 512)]
            bankc = [0]
            with ExitStack() as e1:
                gsrc = fin_g[0:1, :] if last else sm["norm_mix_g"][l + 1:l + 2, :]
                ctxs = make_epi_ctx(e1, gsrc)
                dww = cols_from_rows(e1, "fdw", sm["ffn_dw_w"][l], 3, 2 * DFF)
                dwb = cols_from_rows(e1, "fdb", sm["ffn_dw_b"][l:l + 1, :], 1, 2 * DFF)
                dwbh = S.sb("fdbh", [128, 44, 1], F32, e1)
                S.dve(lambda e: e.tensor_scalar(out=dwbh.ap, in0=dwb.ap, scalar1=0.5, scalar2=None, op0=ALU.mult),
                      reads=[dwb], writes=[dwbh])
                wdn = S.sb("wdn", [128, 22, D], BF16, e1)
                ldw(wdn, wdn.ap, "w_down", l, 0, D)
                wup = [S.sb("wup%d" % i, [128, 2, 8, 128], BF16, e1) for i in range(2)]
                hn = S.sb("fhn", [128, 8, SEG + 2], BF16, e1)
                hT_ = S.sb("fh", [128, 22, SEG], BF16, e1)
                pa = [S.sb("fpa%d" % i, [128, SEG + 2], F32, e1) for i in range(2)]
                ua = [S.sb("fua%d" % i, [128, SEG], F32, e1) for i in range(2)]
                th = S.sb("fth", [128, SEG], F32, e1)
                stage = S.sb("fst", [128, 8, SEG], BF16, e1)
                for s in range(nseq):
                    for g in range(seq_lens[s] // SEG):
                        c0 = hcol0[s] + g * SEG - 1
                        S.dma("sp", hn.ap, hT[2][:, c0:c0 + SEG + 2].rearrange("(c p) n -> p c n", p=128), reads=[d_h[2]],
                              writes=[hn])
                        for j in range(22):
                            w = wup[j % 2]
                            dst_b[0] = w
                            S.dma("sp", w.ap[:, 0], wtile("w_up", l, j), reads=[d_wb[("w_up", l)]], writes=[w])
                            S.dma("sp", w.ap[:, 1], wtile("w_up", l, 22 + j), reads=[d_wb[("w_up", l)]], writes=[w], partial=True)
                            for h in range(2):
                                ch = j + 22 * h
                                for ri, (r0, r1) in enumerate(cranges):
                                    pp = P[bankc[0] % 4]
                                    bankc[0] += 1
                                    S.mm(pp.ap[:, 0:r1 - r0], [(w.ap[:, h, k, :], hn.ap[:, k, r0:r1]) for k in range(8)],
                                         [w, hn], pp)
                                    S.act(lambda e, h=h, pp=pp, r0=r0, r1=r1: e.copy(out=pa[h].ap[:, r0:r1], in_=pp.ap[:, 0:r1 - r0]),
                                          reads=[pp], writes=[pa[h]], partial=(ri > 0))
                                S.act(lambda e, h=h, ch=ch: e.activation(
                                    out=ua[h].ap, in_=pa[h].ap[:, 1:SEG + 1], func=AF.Identity, scale=dww.ap[:, ch, 1:2],
                                    bias=dwb.ap[:, ch, 0:1]), reads=[pa[h], dww, dwb], writes=[ua[h]])
                                for tp in (0, 2):
                                    S.dve(lambda e, h=h, ch=ch, tp=tp: e.scalar_tensor_tensor(
                                        out=ua[h].ap, in0=pa[h].ap[:, tp:tp + SEG], scalar=dww.ap[:, ch, tp:tp + 1],
                                        in1=ua[h].ap, op0=ALU.mult, op1=ALU.add), reads=[pa[h], dww, ua[h]], writes=[ua[h]])
                            S.act(lambda e: e.activation(out=th.ap, in_=ua[0].ap, func=AF.Tanh, scale=0.5), reads=[ua[0]],
                                  writes=[th])
                            S.dve(lambda e: e.scalar_tensor_tensor(out=th.ap, in0=th.ap, scalar=1.0, in1=ua[0].ap,
                                                                   op0=ALU.add, op1=ALU.mult), reads=[th, ua[0]], writes=[th])
                            S.dve(lambda e, j=j: e.scalar_tensor_tensor(out=hT_.ap[:, j, :], in0=th.ap, scalar=0.5,
                                                                        in1=ua[1].ap, op0=ALU.mult, op1=ALU.mult),
                                  reads=[th, ua[1]], writes=[hT_], partial=True)
                        for m in range(SEG // 128):
                            t0 = tok0[s] + g * SEG + m * 128
                            ctx = ctxs[m % 2]
                            pa_, pb_ = P[4], P[5]
                            for hh, pp in ((0, pa_), (1, pb_)):
                                S.mm(pp.ap, [(hT_.ap[:, j, m * 128:(m + 1) * 128], wdn.ap[:, j, hh * 512:(hh + 1) * 512])
                                             for j in range(22)], [hT_, wdn], pp)
                            residual_epilogue(ctx, t0, pa_, pb_, last, stage, m * 128)
                        if not last:
                            cc = hcol0[s] + g * SEG
                            S.dma("pool", hT[0][:, cc:cc + SEG].rearrange("(c p) n -> p c n", p=128), stage.ap,
                                  reads=[stage], writes=[d_h[0]], partial=True)
                S.barrier()

        def phase_copy_h(src, dst):
            with ExitStack() as e1:
                t = S.sb("cph", [128, 8, 512], BF16, e1)
                for s in range(nseq):
                    for g in range(seq_lens[s] // 512):
                        c0 = hcol0[s] + g * 512
                        S.dma("sp", t.ap, hT[src][:, c0:c0 + 512].rearrange("(c p) n -> p c n", p=128), reads=[d_h[src]],
                              writes=[t])
                        S.dma("pool", hT[dst][:, c0:c0 + 512].rearrange("(c p) n -> p c n", p=128), t.ap, reads=[t],
                              writes=[d_h[dst]], partial=True)
                S.barrier()

        def phase_cross(l):
            SEG = 512
            with ExitStack() as e1:
                ctxs = make_epi_ctx(e1, sm["norm_ffn_g"][l:l + 1, :])
                gmem = load_gbc(e1, "gmem", sm["norm_mem_g"][l:l + 1, :])
                wcq = S.sb("wcq", [128, 8, D], BF16, e1)
                wco = S.sb("wco", [128, 8, D], BF16, e1)
                wkv = S.sb("wkv", [128, 8, 2 * D], BF16, e1)
                for wbuf, key in ((wcq, "w_cq"), (wco, "w_co"), (wkv, "w_ckv")):
                    S.dma("sp", wbuf.ap, wb[key][l].rearrange("(c p) n -> p c n", p=128), reads=[d_wb[(key, l)]],
                          writes=[wbuf])
                mnT = S.sb("mnT", [128, 8, MEM], BF16, e1)
                kT = S.sb("ckT", [128, 8, MEM], BF16, e1)
                vtok = S.sb("cvt", [128, 2, D], BF16, e1)
                hn = S.sb("chn", [128, 8, SEG], BF16, e1)
                qT = S.sb("cqT", [128, 8, SEG], BF16, e1)
                oT = S.sb("coT", [128, 8, SEG], BF16, e1)
                pr = [S.sb("cp%d" % i, [128, SEG], BF16, e1) for i in range(2)]
                rden = S.sb("crd", [128, SEG], F32, e1)
                stage = S.sb("cst", [128, 8, SEG], BF16, e1)
                for s in range(nseq):
                    for mm_ in range(2):
                        ctx = ctxs[mm_]
                        S.dma("sp", ctx[0].ap, mem_in[s * MEM + mm_ * 128:s * MEM + (mm_ + 1) * 128, :], writes=[ctx[0]])
                        xn = norm_T(ctx[0], gmem, None, 0, ctx[2])
                        transpose_to(xn, mnT, mm_ * 128, ctx[3])
                    for c in range(8):
                        S.mm(P[5].ap[:, 0:MEM], [(wkv.ap[:, k, c * 128:(c + 1) * 128], mnT.ap[:, k, :]) for k in range(8)],
                             [wkv, mnT], P[5])
                        S.act(lambda e, c=c: e.copy(out=kT.ap[:, c, :], in_=P[5].ap[:, 0:MEM]), reads=[P[5]], writes=[kT],
                              partial=True)
                    for mm_ in range(2):
                        for hh in range(2):
                            S.mm(P[4].ap, [(mnT.ap[:, k, mm_ * 128:(mm_ + 1) * 128],
                                            wkv.ap[:, k, D + hh * 512:D + (hh + 1) * 512]) for k in range(8)], [wkv, mnT], P[4])
                            S.act(lambda e, mm_=mm_, hh=hh: e.copy(out=vtok.ap[:, mm_, hh * 512:(hh + 1) * 512], in_=P[4].ap),
                                  reads=[P[4]], writes=[vtok], partial=True)
                    for g in range(seq_lens[s] // SEG):
                        c0 = hcol0[s] + g * SEG
                        S.dma("sp", hn.ap, hT[1][:, c0:c0 + SEG].rearrange("(c p) n -> p c n", p=128), reads=[d_h[1]],
                              writes=[hn])
                        for c in range(8):
                            S.mm(P[5].ap, [(wcq.ap[:, k, c * 128:(c + 1) * 128], hn.ap[:, k, :]) for k in range(8)],
                                 [wcq, hn], P[5])
                            S.act(lambda e, c=c: e.copy(out=qT.ap[:, c, :], in_=P[5].ap), reads=[P[5]], writes=[qT],
                                  partial=True)
                        for h in range(4):
                            for mm_ in range(2):
                                S.mm(P[mm_].ap, [(kT.ap[:, 2 * h + dc, mm_ * 128:(mm_ + 1) * 128], qT.ap[:, 2 * h + dc, :])
                                                 for dc in range(2)], [kT, qT], P[mm_])
                                S.act(lambda e, mm_=mm_: e.activation(out=pr[mm_].ap, in_=P[mm_].ap, func=AF.Exp,
                                                                      scale=1.0 / 16.0), reads=[P[mm_]], writes=[pr[mm_]])
                            for dc in range(2):
                                S.mm(P[2 + dc].ap, [(vtok.ap[:, mm_, (2 * h + dc) * 128:(2 * h + dc + 1) * 128], pr[mm_].ap)
                                                    for mm_ in range(2)], [vtok, pr[0], pr[1]], P[2 + dc])
                            S.mm(P[4].ap, [(onesB.ap, pr[mm_].ap) for mm_ in range(2)], [onesB, pr[0], pr[1]], P[4])
                            S.dve(lambda e: e.reciprocal(out=rden.ap, in_=P[4].ap), reads=[P[4]], writes=[rden])
                            for dc in range(2):
                                S.dve(lambda e, dc=dc, h=h: e.tensor_tensor(out=oT.ap[:, 2 * h + dc, :], in0=P[2 + dc].ap,
                                                                            in1=rden.ap, op=ALU.mult),
                                      reads=[P[2 + dc], rden], writes=[oT], partial=True)
                        for m in range(SEG // 128):
                            t0 = tok0[s] + g * SEG + m * 128
                            for hh in range(2):
                                S.mm(P[hh].ap, [(oT.ap[:, k, m * 128:(m + 1) * 128], wco.ap[:, k, hh * 512:(hh + 1) * 512])
                                                for k in range(8)], [oT, wco], P[hh])
                            residual_epilogue(ctxs[m % 2], t0, P[0], P[1], False, stage, m * 128)
                        S.dma("pool", hT[2][:, c0:c0 + SEG].rearrange("(c p) n -> p c n", p=128), stage.ap, reads=[stage],
                              writes=[d_h[2]], partial=True)
                S.barrier()

        def branch_conv(l, s, xnT, Sq):
            with ExitStack() as e1:
                cw = cols_from_rows(e1, "cw", sm["conv_dw_w"][l], 31, 512, scale=0.5)
                cbias = cols_from_rows(e1, "cbi", sm["conv_dw_b"][l:l + 1, :], 1, 512)
                lng = cols_from_rows(e1, "clg", sm["conv_ln_g"][l:l + 1, :], 1, 512, scale=0.5)
                lnb = cols_from_rows(e1, "clb", sm["conv_ln_b"][l:l + 1, :], 1, 512, scale=0.5)
                diag = S.sb("cdiag", [128, 4, 31, 128], BF16, e1)
                for i in range(4):
                    for tp in range(31):
                        S.dve(lambda e, i=i, tp=tp: e.tensor_scalar(out=diag.ap[:, i, tp, :], in0=identB.ap,
                                                                    scalar1=cw.ap[:, i, tp:tp + 1], scalar2=None,
                                                                    op0=ALU.mult), reads=[identB, cw], writes=[diag],
                              partial=True)
                cbuf = S.sb("cbuf", [128, 4, Sq + 30], BF16, e1)
                S.pool(lambda e: e.memset(cbuf.ap[:, :, 0:15], 0.0), writes=[cbuf])
                S.pool(lambda e: e.memset(cbuf.ap[:, :, Sq + 15:Sq + 30], 0.0), writes=[cbuf], partial=True)
                wab = [S.sb("cwab%d" % i, [128, 2, 8, 128], BF16, e1) for i in range(2)]
                th = S.sb("cth", [128, 512], F32, e1)
                for i in range(4):
                    w = wab[i % 2]
                    S.dma("sp", w.ap[:, 0], wtile("w_in", l, i), reads=[d_wb[("w_in", l)]], writes=[w])
                    S.dma("sp", w.ap[:, 1], wtile("w_in", l, 4 + i), reads=[d_wb[("w_in", l)]], writes=[w], partial=True)
                    for j in range(Sq // 512):
                        pa_, pb_ = P[(2 * j) % 4], P[(2 * j + 1) % 4]
                        S.mm(pa_.ap, [(w.ap[:, 0, k, :], xnT.ap[:, k, j * 512:(j + 1) * 512]) for k in range(8)], [w, xnT], pa_)
                        S.mm(pb_.ap, [(w.ap[:, 1, k, :], xnT.ap[:, k, j * 512:(j + 1) * 512]) for k in range(8)], [w, xnT], pb_)
                        S.act(lambda e, pb_=pb_: e.activation(out=th.ap, in_=pb_.ap, func=AF.Tanh, scale=0.5), reads=[pb_],
                              writes=[th])
                        S.dve(lambda e, i=i, j=j, pa_=pa_: e.scalar_tensor_tensor(
                            out=cbuf.ap[:, i, 15 + j * 512:15 + (j + 1) * 512], in0=th.ap, scalar=1.0, in1=pa_.ap,
                            op0=ALU.add, op1=ALU.mult), reads=[th, pa_], writes=[cbuf], partial=True)
                cv = S.sb("ccv", [128, 4, 512], F32, e1)
                sq = S.sb("csq", [128, 4, 512], BF16, e1)
                cvb = S.sb("ccvb", [128, 4, 512], BF16, e1)
                m2 = S.sb("cm2", [128, 512], F32, e1)
                rstd = S.sb("crs", [128, 512], F32, e1)
                dd2 = [S.sb("cdd%d" % i_, [128, 512], F32, e1) for i_ in range(2)]
                y22 = [S.sb("cy2%d" % i_, [128, 512], F32, e1) for i_ in range(2)]
                zst = [S.sb("czs%d" % i, [128, 4, 512], BF16, e1) for i in range(2)]
                for j in range(Sq // 512):
                    for i in range(4):
                        pc = P[i % 2]
                        S.mm(pc.ap, [(diag.ap[:, i, tp, :], cbuf.ap[:, i, j * 512 + tp:j * 512 + tp + 512]) for tp in range(31)],
                             [diag, cbuf], pc)
                        S.act(lambda e, i=i, pc=pc: e.activation(out=cv.ap[:, i, :], in_=pc.ap, func=AF.Identity,
                                                                 bias=cbias.ap[:, i, 0:1]), reads=[pc, cbias], writes=[cv],
                              partial=True)
                        S.act(lambda e, i=i: e.activation(out=sq.ap[:, i, :], in_=cv.ap[:, i, :], func=AF.Square),
                              reads=[cv], writes=[sq], partial=True)
                        S.dve(lambda e, i=i: e.tensor_copy(out=cvb.ap[:, i, :], in_=cv.ap[:, i, :]), reads=[cv], writes=[cvb],
                              partial=True)
                    S.mm(P[2].ap, [(onesB512.ap, cvb.ap[:, i, :]) for i in range(4)], [onesB512, cvb], P[2])
                    S.mm(P[3].ap, [(onesB512.ap, sq.ap[:, i, :]) for i in range(4)], [onesB512, sq], P[3])
                    S.act(lambda e: e.activation(out=m2.ap, in_=P[2].ap, func=AF.Square), reads=[P[2]], writes=[m2])
                    S.dve(lambda e: e.scalar_tensor_tensor(out=rstd.ap, in0=P[3].ap, scalar=EPS, in1=m2.ap, op0=ALU.add,
                                                           op1=ALU.subtract), reads=[P[3], m2], writes=[rstd])
                    S.act(lambda e: e.activation(out=rstd.ap, in_=rstd.ap, func=AF.Sqrt), reads=[rstd], writes=[rstd])
                    S.dve(lambda e: e.reciprocal(out=rstd.ap, in_=rstd.ap), reads=[rstd], writes=[rstd])
                    zs = zst[j % 2]
                    for i in range(4):
                        dd = dd2[i % 2]
                        y2 = y22[i % 2]
                        S.dve(lambda e, i=i: e.tensor_tensor(out=dd.ap, in0=cv.ap[:, i, :], in1=P[2].ap, op=ALU.subtract),
                              reads=[cv, P[2]], writes=[dd])
                        S.dve(lambda e: e.tensor_tensor(out=dd.ap, in0=dd.ap, in1=rstd.ap, op=ALU.mult), reads=[dd, rstd],
                              writes=[dd])
                        S.act(lambda e, i=i: e.activation(out=y2.ap, in_=dd.ap, func=AF.Identity, scale=lng.ap[:, i, 0:1],
                                                          bias=lnb.ap[:, i, 0:1]), reads=[dd, lng, lnb], writes=[y2])
                        S.act(lambda e: e.activation(out=dd.ap, in_=y2.ap, func=AF.Tanh), reads=[y2], writes=[dd])
                        S.dve(lambda e, i=i, zs=zs: e.scalar_tensor_tensor(out=zs.ap[:, i, :], in0=dd.ap, scalar=1.0, in1=y2.ap,
                                                                           op0=ALU.add, op1=ALU.mult), reads=[dd, y2],
                              writes=[zs], partial=True)
                    t0 = tok0[s] + j * 512
                    S.dma("pool", zT[0][:, t0:t0 + 512].rearrange("(c p) n -> p c n", p=128), zs.ap, reads=[zs], writes=[d_z[0]],
                          partial=True)
                S.barrier()

        def branch_attn(l, s, xnT, Sq):
            lam_init = 0.8 - 0.6 * math.exp(-0.3 * l)
            nkc = Sq // 128
            nqt = Sq // 512
            with ExitStack() as e1:
                lamr = S.sb("lamr", [128, 256], F32, e1)
                S.dma("sp", lamr.ap, sm["attn_lambda"][l:l + 1].rearrange("o a b -> o (a b)").partition_broadcast(128),
                      writes=[lamr])
                lj = S.sb("lamj", [128, 64], F32, e1)
                ls = S.sb("lams", [128, 2], F32, e1)
                neglam = S.sb("neglam", [128, 1], F32, e1)
                for t in range(2):
                    S.dve(lambda e, t=t: e.scalar_tensor_tensor(out=lj.ap, in0=lamr.ap[:, 128 * t:128 * t + 64], scalar=1.0,
                                                                in1=lamr.ap[:, 128 * t + 64:128 * t + 128], op0=ALU.mult,
                                                                op1=ALU.mult, accum_out=ls.ap[:, t:t + 1]),
                          reads=[lamr], writes=[lj, ls], partial=True)
                S.act(lambda e: e.activation(out=ls.ap, in_=ls.ap, func=AF.Exp), reads=[ls], writes=[ls])
                S.dve(lambda e: e.tensor_tensor(out=neglam.ap, in0=ls.ap[:, 1:2], in1=ls.ap[:, 0:1], op=ALU.subtract),
                      reads=[ls], writes=[neglam])
                S.dve(lambda e: e.tensor_scalar(out=neglam.ap, in0=neglam.ap, scalar1=-lam_init, scalar2=None, op0=ALU.add),
                      reads=[neglam], writes=[neglam])
                sgr = cols_from_rows(e1, "asg", sm["attn_subln_g"][l:l + 1, :], 1, 128, scale=(1.0 - lam_init))
                qT = S.sb("aqT", [128, Sq], BF16, e1)
                kT = S.sb("akT", [128, Sq], BF16, e1)
                vt = S.sb("avt", [128, nkc, 128], BF16, e1)
                wq = [S.sb("awq%d" % i, [128, 8, 128], BF16, e1) for i in range(3)]
                rc = [S.sb("arc%d" % i, [128, 512], F32, e1) for i in range(2)]
                rs = [S.sb("ars%d" % i, [128, 512], F32, e1) for i in range(2)]
                qraw2 = [S.sb("aqr%d" % a_, [128, 512], BF16, e1) for a_ in range(2)]
                t12 = [S.sb("at1%d" % a_, [128, 512], F32, e1) for a_ in range(2)]
                t22 = [S.sb("at2%d" % a_, [128, 512], F32, e1) for a_ in range(2)]
                pr = [[S.sb("apr%d%d" % (a, b), [128, 512], BF16, e1) for b in range(2)] for a in range(2)]
                o0 = S.sb("ao0", [128, 512], F32, e1)
                o1 = S.sb("ao1", [128, 512], F32, e1)
                rd = S.sb("ard", [128, 512], F32, e1)
                pacc = [S.sb("apacc%d" % c, [128, 512], F32, e1) for c in range(2)]
                paccb = [S.sb("apaccb%d" % c, [128, 512], BF16, e1) for c in range(2)]
                osq = S.sb("aosq", [128, 512], BF16, e1)
                zst = [S.sb("azs%d" % i, [128, 512], BF16, e1) for i in range(2)]
                it = 0
                import os
                DBG = int(os.environ.get("DBG_ATTN", "9"))
                for i in range(4 if DBG >= 1 else 0):
                    for a, cb_ in enumerate((8, 12, 16)):
                        S.dma("sp", wq[a].ap, wtile("w_in", l, cb_ + i), reads=[d_wb[("w_in", l)]], writes=[wq[a]])
                    for j in range(nqt if DBG >= 2 else 0):
                        S.dma("sp", rc[j % 2].ap, ropeC[:, j * 512:(j + 1) * 512], writes=[rc[j % 2]])
                        S.dma("sp", rs[j % 2].ap, ropeS[:, j * 512:(j + 1) * 512], writes=[rs[j % 2]])
                        SUB = int(os.environ.get("DBG_SUB", "9"))
                        for a, dstT in ((0, qT), (1, kT)):
                            pj, pm_ = P[4 * a], P[1 + 4 * a]
                            qraw, t1, t2 = qraw2[a], t12[a], t22[a]
                            S.mm(pj.ap, [(wq[a].ap[:, k, :], xnT.ap[:, k, j * 512:(j + 1) * 512]) for k in range(8)],
                                 [wq[a], xnT], pj)
                            S.act(lambda e, qraw=qraw, pj=pj: e.copy(out=qraw.ap, in_=pj.ap), reads=[pj], writes=[qraw])
                            S.mm(pm_.ap, [(permB.ap, qraw.ap)], [permB, qraw], pm_)
                            S.dve(lambda e, j=j, t1=t1, pj=pj: e.tensor_tensor(out=t1.ap, in0=pj.ap, in1=rc[j % 2].ap, op=ALU.mult),
                                  reads=[pj, rc[j % 2]], writes=[t1])
                            S.dve(lambda e, j=j, t2=t2, pm_=pm_: e.tensor_tensor(out=t2.ap, in0=pm_.ap, in1=rs[j % 2].ap, op=ALU.mult),
                                  reads=[pm_, rs[j % 2]], writes=[t2])
                            S.pool(lambda e, j=j, dstT=dstT, t1=t1, t2=t2: e.tensor_tensor(out=dstT.ap[:, j * 512:(j + 1) * 512], in0=t1.ap,
                                                                                           in1=t2.ap, op=ALU.add), reads=[t1, t2],
                                   writes=[dstT], partial=True)
                    for m in range(nkc if DBG >= 3 else 0):
                        pv = P[2 + (m % 2)]
                        S.mm(pv.ap[:, 0:128], [(xnT.ap[:, k, m * 128:(m + 1) * 128], wq[2].ap[:, k, :]) for k in range(8)],
                             [wq[2], xnT], pv)
                        S.act(lambda e, m=m, pv=pv: e.copy(out=vt.ap[:, m, :], in_=pv.ap[:, 0:128]), reads=[pv], writes=[vt],
                              partial=True)
                    for jq in range(nqt if DBG >= 4 else 0):
                        qs = slice(jq * 512, (jq + 1) * 512)
                        def emit_scores(kc):
                            ks = slice(kc * 128, (kc + 1) * 128)
                            pp = pr[kc % 2]
                            for c in range(2):
                                psc = P[c + 4 * (kc % 2)]
                                S.mm(psc.ap, [(kT.ap[c * 64:(c + 1) * 64, ks], qT.ap[c * 64:(c + 1) * 64, qs])], [kT, qT], psc)
                                S.act(lambda e, c=c, pp=pp, psc=psc: e.activation(out=pp[c].ap, in_=psc.ap, func=AF.Exp, scale=0.125),
                                      reads=[psc], writes=[pp[c]])

                        emit_scores(0)
                        for kc in range(nkc):
                            if kc + 1 < nkc:
                                emit_scores(kc + 1)
                            pp = pr[kc % 2]
                            for c in range(2):
                                S.pe(lambda e, c=c, kc=kc, pp=pp: e.matmul(P[2 + c].ap, lhsT=vt.ap[:, kc, :], rhs=pp[c].ap,
                                                                           start=(kc == 0), stop=(kc == nkc - 1)),
                                     reads=[vt, pp[c]], writes=[P[2 + c]])
                                eng = S.dve
                                if kc == 0:
                                    eng(lambda e, c=c, pp=pp: e.tensor_copy(out=pacc[c].ap, in_=pp[c].ap), reads=[pp[c]],
                                        writes=[pacc[c]])
                                else:
                                    eng(lambda e, c=c, pp=pp: e.tensor_tensor(out=pacc[c].ap, in0=pacc[c].ap, in1=pp[c].ap,
                                                                              op=ALU.add), reads=[pp[c], pacc[c]], writes=[pacc[c]])
                        for c in range(2):
                            S.act(lambda e, c=c: e.copy(out=paccb[c].ap, in_=pacc[c].ap), reads=[pacc[c]], writes=[paccb[c]])
                            S.mm(P[c].ap, [(onesB.ap, paccb[c].ap)], [onesB, paccb[c]], P[c])
                        S.dve(lambda e: e.reciprocal(out=rd.ap, in_=P[0].ap), reads=[P[0]], writes=[rd])
                        S.dve(lambda e: e.tensor_tensor(out=o0.ap, in0=P[2].ap, in1=rd.ap, op=ALU.mult), reads=[P[2], rd],
                              writes=[o0])
                        S.dve(lambda e: e.reciprocal(out=rd.ap, in_=P[1].ap), reads=[P[1]], writes=[rd])
                        S.dve(lambda e: e.tensor_tensor(out=o1.ap, in0=P[3].ap, in1=rd.ap, op=ALU.mult), reads=[P[3], rd],
                              writes=[o1])
                        S.dve(lambda e: e.scalar_tensor_tensor(out=o0.ap, in0=o1.ap, scalar=neglam.ap, in1=o0.ap, op0=ALU.mult,
                                                               op1=ALU.add), reads=[o1, neglam, o0], writes=[o0])
                        S.act(lambda e: e.activation(out=osq.ap, in_=o0.ap, func=AF.Square), reads=[o0], writes=[osq])
                        S.mm(P[0].ap, [(onesB128.ap, osq.ap)], [onesB128, osq], P[0])
                        S.act(lambda e: e.activation(out=rd.ap, in_=P[0].ap, func=AF.Sqrt, bias=epsc.ap[:, 0:1]),
                              reads=[P[0], epsc], writes=[rd])
                        S.dve(lambda e: e.reciprocal(out=rd.ap, in_=rd.ap), reads=[rd], writes=[rd])
                        S.dve(lambda e: e.tensor_tensor(out=o0.ap, in0=o0.ap, in1=rd.ap, op=ALU.mult), reads=[o0, rd],
                              writes=[o0])
                        zs = zst[it % 2]
                        it += 1
                        S.act(lambda e, zs=zs: e.activation(out=zs.ap, in_=o0.ap, func=AF.Identity, scale=sgr.ap[:, 0, 0:1]),
                              reads=[o0, sgr], writes=[zs])
                        t0 = tok0[s] + jq * 512
                        S.dma("pool", zT[1][i * 128:(i + 1) * 128, t0:t0 + 512], zs.ap, reads=[zs], writes=[d_z[1]], partial=True)
                S.barrier()

        def branch_hgrn(l, s, xnT, Sq):
            nt = Sq // 128
            nch = Sq // 64
            nq = Sq // 512
            with ExitStack() as e1:
                ngr = cols_from_rows(e1, "hng", sm["hg_norm_g"][l:l + 1, :], 1, 128, scale=0.5)
                q2 = S.sb("hq2", [128, Sq], F32, e1)
                dual = Sq <= 2048
                dsets = []
                for di in range(2 if dual else 1):
                    dsets.append((S.sb("hB2", [128, Sq], F32, e1), S.sb("hB34", [128, 2 * Sq], F32, e1),
                                  S.sb("hqt", [128, Sq], BF16, e1), S.sb("hkt", [128, Sq], BF16, e1),
                                  S.sb("hkk", [128, nt, 128], BF16, e1), S.sb("hAm", [128, 4, 64], BF16, e1),
                                  S.sb("hbnd", [128, 3, nch], F32, e1), S.sb("hdA", [128, nch], F32, e1),
                                  S.sb("hdB", [128, nch], F32, e1), S.sb("hdC", [128, nch], F32, e1)))
                vtok = S.sb("hvt", [128, nt, 128], BF16, e1)
                osum = S.sb("hos", [128, Sq], F32, e1)
                one1 = S.sb("hone", [128, 1], F32, e1)
                S.dve(lambda e: e.memset(one1.ap, 1.0), writes=[one1])
                wv = [S.sb("hw%d" % i, [128, 8, 128], BF16, e1) for i in range(4)]
                th2 = [S.sb("hth%d" % i_, [128, 512], F32, e1) for i_ in range(2)]
                t52 = [S.sb("ht5%d" % i_, [128, 512], F32, e1) for i_ in range(2)]
                t5b2 = [S.sb("ht5b%d" % i_, [128, 512], BF16, e1) for i_ in range(2)]
                th = th2[0]
                zst = [S.sb("hzs%d" % i, [128, 512], BF16, e1) for i in range(2)]
                wi = [0]

                def loadw(col):
                    w = wv[wi[0] % 4]
                    wi[0] += 1
                    S.dma("sp", w.ap, wtile("w_in", l, col // 128), reads=[d_wb[("w_in", l)]], writes=[w])
                    return w

                def proj_tile(w, j, pp):
                    S.mm(pp.ap, [(w.ap[:, k, :], xnT.ap[:, k, j * 512:(j + 1) * 512]) for k in range(8)], [w, xnT], pp)

                for i in range(4):
                    w = loadw(2560 + i * 128)
                    for j in range(nq):
                        pp = P[j % 2]
                        proj_tile(w, j, pp)
                        S.act(lambda e, pp=pp: e.activation(out=th.ap, in_=pp.ap, func=AF.Tanh, scale=0.5), reads=[pp], writes=[th])
                        S.dve(lambda e, j=j, pp=pp: e.scalar_tensor_tensor(out=q2.ap[:, j * 512:(j + 1) * 512], in0=th.ap,
                                                                           scalar=1.0, in1=pp.ap, op0=ALU.add, op1=ALU.mult),
                              reads=[th, pp], writes=[q2], partial=True)
                    w = loadw(4096 + i * 128)
                    for m in range(nt):
                        pv = P[2 + (m % 2)]
                        S.mm(pv.ap[:, 0:128], [(xnT.ap[:, k, m * 128:(m + 1) * 128], w.ap[:, k, :]) for k in range(8)], [w, xnT], pv)
                        S.act(lambda e, m=m, pv=pv: e.copy(out=vtok.ap[:, m, :], in_=pv.ap[:, 0:128]), reads=[pv], writes=[vtok],
                              partial=True)
                    def dir_gen(d, B2, B34, qtl, ktl, ktok, Am, bnd, dA, dB, dC):
                        B3 = B34.ap[:, 0:Sq]
                        B4 = B34.ap[:, Sq:2 * Sq]
                        Uall = B34.ap.rearrange("p (c v) -> p c v", v=128)
                        Sbf = Buf("hSb_alias")
                        Sbf.ap = B2.ap.bitcast(BF16).rearrange("p (c v) -> p c v", v=128)
                        lbcol = lbs.ap[:, l, d, i:i + 1]
                        yield
                        omcol = omlb.ap[:, l, d, i:i + 1]
                        yield
                        w = loadw((3072 if d == 0 else 3584) + i * 128)
                        yield
                        for j in range(nq):
                            pp = P[j % 2]
                            proj_tile(w, j, pp)
                            S.act(lambda e, j=j, pp=pp: e.activation(out=B2.ap[:, j * 512:(j + 1) * 512], in_=pp.ap, func=AF.Exp,
                                                                     scale=-1.0), reads=[pp], writes=[B2], partial=True)
                            yield
                        yield
                        S.act(lambda e: e.activation(out=B3, in_=B2.ap, func=AF.Ln, bias=1.0), reads=[B2], writes=[B34])
                        yield
                        S.act(lambda e: e.activation(out=B4, in_=B2.ap, func=AF.Ln, bias=1.0, scale=lbcol), reads=[B2, lbs],
                              writes=[B34], partial=True)
                        yield
                        S.dve(lambda e: e.tensor_tensor(out=B4, in0=B4, in1=B3, op=ALU.subtract), reads=[B34], writes=[B34])
                        yield
                        S.act(lambda e: e.activation(out=B3, in_=B3, func=AF.Exp, scale=-1.0), reads=[B34], writes=[B34])
                        yield
                        S.dve(lambda e: e.scalar_tensor_tensor(out=B2.ap, in0=B2.ap, scalar=omcol, in1=B3, op0=ALU.mult,
                                                               op1=ALU.mult), reads=[B2, omlb, B34], writes=[B2])
                        yield
                        S.dve(lambda e: e.tensor_tensor_scan(out=B3, data0=one1.ap[:, 0:1].to_broadcast([128, Sq]), data1=B4,
                                                             initial=0.0, op0=ALU.mult, op1=ALU.add), reads=[B34, one1],
                              writes=[B34])
                        yield
                        if d == 1:
                            S.dve(lambda e: e.tensor_tensor(out=B3, in0=B3, in1=B4, op=ALU.subtract), reads=[B34], writes=[B34])
                        yield
                        G3 = B3.rearrange("p (c n) -> p c n", n=64)
                        yield
                        S.dve(lambda e: e.tensor_copy(out=bnd.ap[:, 0, :], in_=G3[:, :, 31]), reads=[B34], writes=[bnd])
                        yield
                        if d == 0:
                            S.dve(lambda e: e.tensor_copy(out=bnd.ap[:, 1, :], in_=G3[:, :, 63]), reads=[B34], writes=[bnd],
                                  partial=True)
                            S.dve(lambda e: e.memset(bnd.ap[:, 2, 0:1], 0.0), writes=[bnd], partial=True)
                            if nch > 1:
                                S.dve(lambda e: e.tensor_copy(out=bnd.ap[:, 2, 1:nch], in_=G3[:, 0:nch - 1, 63]), reads=[B34],
                                      writes=[bnd], partial=True)
                        else:
                            g3 = B4.rearrange("p (c n) -> p c n", n=64)
                            S.dve(lambda e: e.tensor_tensor(out=bnd.ap[:, 1, :], in0=G3[:, :, 63], in1=g3[:, :, 63], op=ALU.add),
                                  reads=[B34], writes=[bnd], partial=True)
                            S.dve(lambda e: e.tensor_copy(out=bnd.ap[:, 2, :], in_=G3[:, :, 0]), reads=[B34], writes=[bnd],
                                  partial=True)
                        yield
                        if d == 0:
                            prs = ((dA, 0, 2), (dB, 1, 0), (dC, 1, 2))
                        else:
                            prs = ((dA, 1, 0), (dB, 0, 2), (dC, 1, 2))
                        yield
                        for dst, a_, b_ in prs:
                            S.dve(lambda e, dst=dst, a_=a_, b_=b_: e.tensor_tensor(out=dst.ap, in0=bnd.ap[:, a_, :],
                                                                                   in1=bnd.ap[:, b_, :], op=ALU.subtract),
                                  reads=[bnd], writes=[dst])
                            S.act(lambda e, dst=dst: e.activation(out=dst.ap, in_=dst.ap, func=AF.Exp), reads=[dst], writes=[dst])
                            yield
                        yield
                        S.dve(lambda e: e.tensor_tensor(out=G3, in0=G3, in1=bnd.ap[:, 0, :].unsqueeze(2).to_broadcast([128, nch, 64]),
                                                        op=ALU.subtract), reads=[B34, bnd], writes=[B34])
                        yield
                        sgn = 1.0 if d == 0 else -1.0
                        yield
                        S.act(lambda e: e.activation(out=B4, in_=B3, func=AF.Exp, scale=sgn), reads=[B34], writes=[B34])
                        yield
                        S.dve(lambda e: e.scalar_tensor_tensor(out=qtl.ap, in0=q2.ap, scalar=0.5, in1=B4, op0=ALU.mult,
                                                               op1=ALU.mult), reads=[q2, B34], writes=[qtl])
                        yield
                        S.act(lambda e: e.activation(out=B4, in_=B3, func=AF.Exp, scale=-sgn), reads=[B34], writes=[B34])
                        yield
                        S.dve(lambda e: e.tensor_tensor(out=ktl.ap, in0=B2.ap, in1=B4, op=ALU.mult), reads=[B2, B34], writes=[ktl])
                        yield
                        for m in range(nt):
                            tb = T[m % 2]
                            S.pe(lambda e, m=m, tb=tb: e.transpose(out=tb.ap[:, 0:128], in_=ktl.ap[:, m * 128:(m + 1) * 128],
                                                                   identity=identB.ap), reads=[ktl, identB], writes=[tb])
                            S.act(lambda e, m=m, tb=tb: e.copy(out=ktok.ap[:, m, :], in_=tb.ap[:, 0:128]), reads=[tb],
                                  writes=[ktok], partial=True)
                            yield
                        yield
                        Uall4 = Uall.rearrange("p (t h) v -> p t h v", h=2)
                        yield
                        dB2 = dB.ap.rearrange("p (t h) -> p t h", h=2)
                        yield
                        for t4 in range(0, nt, 4):
                            for tt in range(4):
                                m = t4 + tt
                                for hh in range(2):
                                    pu = P[4 + hh]
                                    S.pe(lambda e, tt=tt, m=m, hh=hh, pu=pu: e.matmul(
                                        pu.ap[:, tt * 128:(tt + 1) * 128], lhsT=ktok.ap[hh * 64:(hh + 1) * 64, m, :],
                                        rhs=vtok.ap[hh * 64:(hh + 1) * 64, m, :], start=True, stop=True),
                                        reads=[ktok, vtok], writes=[pu], inc=(tt == 3))
                            for hh in range(2):
                                pu = P[4 + hh]
                                S.dve(lambda e, t4=t4, hh=hh, pu=pu: e.tensor_tensor(
                                    out=Uall4[:, t4:t4 + 4, hh, :], in0=pu.ap.rearrange("p (c v) -> p c v", v=128),
                                    in1=dB2[:, t4:t4 + 4, hh].unsqueeze(2).to_broadcast([128, 4, 128]), op=ALU.mult),
                                    reads=[pu, dB], writes=[B34], partial=True)
                            yield
                        yield
                        order = list(range(nch)) if d == 0 else list(range(nch - 1, -1, -1))
                        yield
                        for idx in range(1, nch):
                            c, cp = order[idx], order[idx - 1]
                            S.dve(lambda e, c=c, cp=cp: e.scalar_tensor_tensor(out=Uall[:, c, :], in0=Uall[:, cp, :],
                                                                               scalar=dC.ap[:, c:c + 1], in1=Uall[:, c, :],
                                                                               op0=ALU.mult, op1=ALU.add),
                                  reads=[B34, dC], writes=[B34])
                            yield
                        yield
                        first = order[0]
                        yield
                        S.dve(lambda e: e.memset(Sbf.ap[:, first, :], 0.0), writes=[B2])
                        yield
                        if nch > 1:
                            if d == 0:
                                S.dve(lambda e: e.tensor_tensor(out=Sbf.ap[:, 1:nch, :], in0=Uall[:, 0:nch - 1, :],
                                                                in1=dA.ap[:, 1:nch].unsqueeze(2).to_broadcast([128, nch - 1, 128]),
                                                                op=ALU.mult), reads=[B34, dA], writes=[B2], partial=True)
                            else:
                                S.dve(lambda e: e.tensor_tensor(out=Sbf.ap[:, 0:nch - 1, :], in0=Uall[:, 1:nch, :],
                                                                in1=dA.ap[:, 0:nch - 1].unsqueeze(2).to_broadcast([128, nch - 1, 128]),
                                                                op=ALU.mult), reads=[B34, dA], writes=[B2], partial=True)
                        yield
                        msk = maskF if d == 0 else maskBk
                        yield
                        for j in range(nq):
                            pa_ = P[j % 2]
                            po = P[2 + (j % 2)]
                            for mm_ in range(4):
                                m = j * 4 + mm_
                                for hh in range(2):
                                    c = 2 * m + hh
                                    S.pe(lambda e, c=c, hh=hh, mm_=mm_, pa_=pa_: e.matmul(
                                        pa_.ap[hh * 64:(hh + 1) * 64, mm_ * 64:(mm_ + 1) * 64], lhsT=ktl.ap[:, c * 64:(c + 1) * 64],
                                        rhs=qtl.ap[:, c * 64:(c + 1) * 64], start=True, stop=True), reads=[ktl, qtl], writes=[pa_],
                                        inc=(mm_ == 3 and hh == 1))
                            S.dve(lambda e, pa_=pa_: e.scalar_tensor_tensor(
                                out=Am.ap, in0=pa_.ap[:, 0:256].rearrange("p (m t) -> p m t", t=64), scalar=1e30,
                                in1=msk.ap.unsqueeze(1).to_broadcast([128, 4, 64]), op0=ALU.min, op1=ALU.mult),
                                reads=[pa_, msk], writes=[Am])
                            for mm_ in range(4):
                                m = j * 4 + mm_
                                for hh in range(2):
                                    c = 2 * m + hh
                                    cs = slice((mm_ * 2 + hh) * 64, (mm_ * 2 + hh + 1) * 64)
                                    S.pe(lambda e, m=m, hh=hh, mm_=mm_, cs=cs, po=po: e.matmul(
                                        po.ap[:, cs], lhsT=vtok.ap[hh * 64:(hh + 1) * 64, m, :], rhs=Am.ap[hh * 64:(hh + 1) * 64, mm_, :],
                                        start=True, stop=False), reads=[vtok, Am], writes=[po], inc=False)
                                    S.pe(lambda e, c=c, cs=cs, po=po: e.matmul(
                                        po.ap[:, cs], lhsT=Sbf.ap[:, c, :], rhs=qtl.ap[:, c * 64:(c + 1) * 64], start=False, stop=True),
                                        reads=[B2, qtl], writes=[po], inc=(mm_ == 3 and hh == 1))
                            S.dve(lambda e, j=j, po=po: e.tensor_tensor(out=osum.ap[:, j * 512:(j + 1) * 512],
                                                                        in0=osum.ap[:, j * 512:(j + 1) * 512], in1=po.ap,
                                                                        op=ALU.add), reads=[po, osum], writes=[osum])
                            yield
                    S.pool(lambda e: e.memset(osum.ap, 0.0), writes=[osum])
                    if dual:
                        g0 = dir_gen(0, *dsets[0])
                        g1 = dir_gen(1, *dsets[1])
                        alive = [g0, g1]
                        while alive:
                            for g_ in list(alive):
                                try:
                                    next(g_)
                                except StopIteration:
                                    alive.remove(g_)
                    else:
                        for d in range(2):
                            for _ in dir_gen(d, *dsets[0]):
                                pass
                    w = loadw(4608 + i * 128)
                    for j in range(nq):
                        js = slice(j * 512, (j + 1) * 512)
                        pp = P[j % 2]
                        th = th2[j % 2]
                        t5 = t52[j % 2]
                        t5b = t5b2[j % 2]
                        proj_tile(w, j, pp)
                        S.act(lambda e, pp=pp: e.activation(out=th.ap, in_=pp.ap, func=AF.Tanh, scale=0.5), reads=[pp], writes=[th])
                        S.dve(lambda e, pp=pp: e.scalar_tensor_tensor(out=th.ap, in0=th.ap, scalar=1.0, in1=pp.ap, op0=ALU.add,
                                                                      op1=ALU.mult), reads=[th, pp], writes=[th])
                        S.act(lambda e, js=js: e.activation(out=t5b.ap, in_=osum.ap[:, js], func=AF.Square), reads=[osum], writes=[t5b])
                        ps_ = P[2 + (j % 2)]
                        S.mm(ps_.ap, [(onesB128.ap, t5b.ap)], [onesB128, t5b], ps_)
                        S.act(lambda e, ps_=ps_: e.activation(out=t5.ap, in_=ps_.ap, func=AF.Sqrt, bias=epsc.ap[:, 0:1]),
                              reads=[ps_, epsc], writes=[t5])
                        S.dve(lambda e: e.reciprocal(out=t5.ap, in_=t5.ap), reads=[t5], writes=[t5])
                        S.dve(lambda e, js=js: e.tensor_tensor(out=t5.ap, in0=t5.ap, in1=osum.ap[:, js], op=ALU.mult),
                              reads=[t5, osum], writes=[t5])
                        S.dve(lambda e: e.tensor_tensor(out=t5.ap, in0=t5.ap, in1=th.ap, op=ALU.mult), reads=[t5, th], writes=[t5])
                        zs = zst[j % 2]
                        S.act(lambda e, zs=zs: e.activation(out=zs.ap, in_=t5.ap, func=AF.Identity, scale=ngr.ap[:, 0, 0:1]),
                              reads=[t5, ngr], writes=[zs])
                        t0 = tok0[s] + j * 512
                        S.dma("pool", zT[2][i * 128:(i + 1) * 128, t0:t0 + 512], zs.ap, reads=[zs], writes=[d_z[2]], partial=True)
                S.barrier()

        def merge(l, s, xnT, Sq, active):
            with ExitStack() as e1:
                ctxs = make_epi_ctx(e1, sm["norm_cross_g"][l:l + 1, :])
                wout = []
                for b, key in enumerate(("w_conv_out", "w_attn_out", "w_hg_out")):
                    wt = S.sb("mwo%d" % b, [128, 4, D], BF16, e1)
                    S.dma("sp", wt.ap, wb[key][l].rearrange("(c p) n -> p c n", p=128), reads=[d_wb[(key, l)]], writes=[wt])
                    wout.append(wt)
                wo = S.sb("mwo", [128, 8, D], BF16, e1)
                S.dma("sp", wo.ap, wb["w_o"][l].rearrange("(c p) n -> p c n", p=128), reads=[d_wb[("w_o", l)]], writes=[wo])
                resident = Sq <= 2048
                if resident:
                    wgall = S.sb("mwgall", [128, 8, 3, 8, 128], BF16, e1)
                    for c in range(8):
                        for b in active:
                            S.dma("sp", wgall.ap[:, c, b], wtile("w_in", l, 40 + b * 8 + c), reads=[d_wb[("w_in", l)]],
                                  writes=[wgall], partial=True)
                    wg = None
                else:
                    wg = [S.sb("mwg%d" % i, [128, 3, 8, 128], BF16, e1) for i in range(2)]
                zt = [S.sb("mz%d" % b, [128, 4, 512], BF16, e1) for b in range(3)]
                thb = [S.sb("mth%d" % b, [128, 512], F32, e1) for b in range(3)]
                mb = [S.sb("mmb%d" % b, [128, 512], F32, e1) for b in range(3)]
                mg = S.sb("mmg", [128, 8, 512], BF16, e1)
                stage = S.sb("mst", [128, 8, 512], BF16, e1)
                for g in range(Sq // 512):
                    t0 = tok0[s] + g * 512
                    gs = slice(g * 512, (g + 1) * 512)
                    for b in active:
                        S.dma("sp", zt[b].ap, zT[b][:, t0:t0 + 512].rearrange("(c p) n -> p c n", p=128), reads=[d_z[b]],
                              writes=[zt[b]])
                    for c in range(8):
                        if resident:
                            w = Buf("wgview")
                            w = wgall
                            wap = wgall.ap[:, c]
                        else:
                            w = wg[c % 2]
                            wap = w.ap
                            for b in active:
                                col = 5120 + b * 1024 + c * 128
                                S.dma("sp", w.ap[:, b], wtile("w_in", l, col // 128), reads=[d_wb[("w_in", l)]], writes=[w],
                                      partial=True)
                        for b in active:
                            py, pg = P[2 * b], P[2 * b + 1]
                            S.mm(py.ap, [(wout[b].ap[:, k, c * 128:(c + 1) * 128], zt[b].ap[:, k, :]) for k in range(4)],
                                 [wout[b], zt[b]], py)
                            S.mm(pg.ap, [(wap[:, b, k, :], xnT.ap[:, k, gs]) for k in range(8)], [w, xnT], pg)
                            th = thb[b]
                            S.act(lambda e, pg=pg, th=th: e.activation(out=th.ap, in_=pg.ap, func=AF.Tanh, scale=0.5), reads=[pg],
                                  writes=[th])
                            S.dve(lambda e, b=b, py=py, th=th: e.scalar_tensor_tensor(out=mb[b].ap, in0=th.ap, scalar=1.0, in1=py.ap,
                                                                                      op0=ALU.add, op1=ALU.mult), reads=[th, py],
                                  writes=[mb[b]])
                        acc = mb[active[0]]
                        for b in active[1:-1]:
                            S.dve(lambda e, b=b, acc=acc: e.tensor_tensor(out=acc.ap, in0=acc.ap, in1=mb[b].ap, op=ALU.add),
                                  reads=[acc, mb[b]], writes=[acc])
                        if len(active) > 1:
                            S.dve(lambda e, c=c, acc=acc: e.tensor_tensor(out=mg.ap[:, c, :], in0=acc.ap, in1=mb[active[-1]].ap,
                                                                          op=ALU.add), reads=[acc, mb[active[-1]]], writes=[mg],
                                  partial=True)
                        else:
                            S.dve(lambda e, c=c, acc=acc: e.tensor_copy(out=mg.ap[:, c, :], in_=acc.ap), reads=[acc], writes=[mg],
                                  partial=True)
                    for m in range(4):
                        for hh in range(2):
                            S.mm(P[hh].ap, [(mg.ap[:, k, m * 128:(m + 1) * 128], wo.ap[:, k, hh * 512:(hh + 1) * 512])
                                            for k in range(8)], [mg, wo], P[hh])
                        residual_epilogue(ctxs[m % 2], t0 + m * 128, P[0], P[1], False, stage, m * 128, scale=0.5)
                    c0 = hcol0[s] + g * 512
                    S.dma("pool", hT[1][:, c0:c0 + 512].rearrange("(c p) n -> p c n", p=128), stage.ap, reads=[stage],
                          writes=[d_h[1]], partial=True)
                S.barrier()

        def phase_mix(l):
            for s in range(nseq):
                Sq = seq_lens[s]
                with ExitStack() as e0:
                    xnT = S.sb("xnT", [128, 8, Sq], BF16, e0)
                    S.dma("sp", xnT.ap, hT[0][:, hcol0[s]:hcol0[s] + Sq].rearrange("(c p) n -> p c n", p=128), reads=[d_h[0]],
                          writes=[xnT])
                    active = []
                    if "conv" in flags:
                        branch_conv(l, s, xnT, Sq)
                        active.append(0)
                    if "attn" in flags:
                        branch_attn(l, s, xnT, Sq)
                        active.append(1)
                    if "hgrn" in flags:
                        branch_hgrn(l, s, xnT, Sq)
                        active.append(2)
                    merge(l, s, xnT, Sq, active)
                    S.barrier()

        MIX = phase_mix
        CROSS = phase_cross
        S.barrier()
        phase_prologue()
        for l in range(L):
            if MIX is not None and any(f in flags for f in ("conv", "attn", "hgrn")):
                MIX(l)
            else:
                phase_copy_h(0, 1)
            if CROSS is not None and "cross" in flags:
                CROSS(l)
            else:
                phase_copy_h(1, 2)
            phase_ffn(l)
        S.barrier()
        print("instructions emitted:", S.ninst)
    return nc


def rope_tables(smax):
    inv = 1.0 / (500000.0 ** (np.arange(0, 16, 2, dtype=np.float32) / 16.0))
    ang = np.arange(smax, dtype=np.float32)[None, :] * inv[:, None].astype(np.float32)
    C = np.ones((128, smax), np.float32)
    Sg = np.zeros((128, smax), np.float32)
    for blk in (0, 64):
        C[blk:blk + 8] = np.cos(ang)
        C[blk + 8:blk + 16] = np.cos(ang)
        Sg[blk:blk + 8] = -np.sin(ang)
        Sg[blk + 8:blk + 16] = np.sin(ang)
    Pm = np.zeros((128, 128), np.float32)
    for blk in (0, 64):
        for d in range(8):
            Pm[blk + d + 8, blk + d] = 1.0
            Pm[blk + d, blk + d + 8] = 1.0
    return C, Sg, Pm


def make_in_maps(seq_assign, xs_list, mems_list, params, smax):
    C, Sg, Pm = rope_tables(smax)
    tiled = {}
    for k in ("w_in", "w_up"):
        w = np.asarray(params[k], dtype=np.float32)
        Lw, Kw, Nw = w.shape
        tiled[k] = np.ascontiguousarray(w.reshape(Lw, 8, 128, Nw // 128, 128).transpose(0, 3, 2, 1, 4)).reshape(
            Lw, Nw // 128, 128, 1024)
    maps = []
    for c in range(len(seq_assign)):
        m = {"x": np.ascontiguousarray(np.concatenate([xs_list[i] for i in seq_assign[c]], 0)),
             "mem": np.ascontiguousarray(np.concatenate([mems_list[i] for i in seq_assign[c]], 0)),
             "ropeC": C, "ropeS": Sg, "permM": Pm}
        for k in list(WSHAPES) + list(SMALL) + ["hg_lb_param"]:
            m[k] = tiled.get(k) if k in tiled else np.ascontiguousarray(params[k], dtype=np.float32)
        m["final_norm_g"] = np.ascontiguousarray(params["final_norm_g"], dtype=np.float32).reshape(1, D)
        maps.append(m)
    return maps


def kernel(**inputs):
    xp = np.asarray(inputs["x_prompt"], np.float32)
    xsm = np.asarray(inputs["x_sample"], np.float32)
    mp = np.asarray(inputs["mem_prompt"], np.float32)
    ms = np.asarray(inputs["mem_sample"], np.float32)
    depth = inputs["w_in"].shape[0]
    nb, sp = xp.shape[0], xp.shape[1]
    nsb, ssm = xsm.shape[0], xsm.shape[1]
    seqs = [xp[i] for i in range(nb)] + [xsm[i] for i in range(nsb)]
    mems = [mp[i] for i in range(nb)] + [ms[i] for i in range(nsb)]
    n = 8
    per = nb // n
    assign = [[c * per + i for i in range(per)] + [nb + (c % nsb)] for c in range(n)]
    seq_lens = [sp] * per + [ssm]
    nc = build(seq_lens, depth)
    maps = make_in_maps(assign, seqs, mems, inputs, max(sp, ssm))
    res = run_bass_kernel_spmd(nc, maps, core_ids=list(range(n)))
    yp = np.zeros_like(xp)
    ysm = np.zeros_like(xsm)
    for c in range(n):
        y = res.results[c]["y"]
        o = 0
        for i in assign[c]:
            ln = seqs[i].shape[0]
            if i < nb:
                yp[i] = y[o:o + ln]
            elif c < nsb:
                ysm[i - nb] = y[o:o + ln]
            o += ln
    return (yp, ysm)
```
